# Optimizing a Trainium2 kernel written in Bass

```python
import math
import jax, jax.numpy as jnp
from jax import lax
import numpy as np

D_MODEL = 1024
BATCH = 4
SEQ = 4096
DEPTH = 2

GRID_W = 64
CTX_LEN = 256
EPS = 1e-6
ROPE_THETA = 10000.0
Q_BLOCK = 128
FOURIER_WIDTH = D_MODEL // 4
FOURIER_GROUPS = 4
FOURIER_GROUP_DIM = FOURIER_WIDTH // FOURIER_GROUPS
DIFF_WIDTH = D_MODEL - FOURIER_WIDTH
DIFF_HEAD_DIM = 64
DIFF_V_DIM = 2 * DIFF_HEAD_DIM
DIFF_HEADS = DIFF_WIDTH // DIFF_V_DIM
IN_WIDTH_EVEN = 3 * DIFF_WIDTH + FOURIER_WIDTH
ATTN_SCALE = DIFF_HEAD_DIM ** -0.5
LAMBDA_STD = 0.1
CONV_WIDTH = 31
D_FF = -(-8 * D_MODEL // (3 * 256)) * 256

kernel_name = 'hybrid_diff_fourier_conformer_dit'


def rmsnorm(x, g):
    x32 = x.astype(jnp.float32)
    y = x32 * lax.rsqrt(jnp.mean(x32 * x32, axis=-1, keepdims=True) + EPS)
    return (y * g.astype(jnp.float32)).astype(x.dtype)


def layernorm(x, g, b):
    x32 = x.astype(jnp.float32)
    mu = jnp.mean(x32, axis=-1, keepdims=True)
    xc = x32 - mu
    y = xc * lax.rsqrt(jnp.mean(xc * xc, axis=-1, keepdims=True) + EPS)
    return (y * g.astype(jnp.float32) + b.astype(jnp.float32)).astype(x.dtype)


def modulate(h, shift, scale):
    return h * (1 + scale) + shift


def axial_rope_tables(rows_n):
    row = jnp.repeat(jnp.arange(rows_n, dtype=jnp.float32), GRID_W)
    col = jnp.tile(jnp.arange(GRID_W, dtype=jnp.float32), rows_n)
    axis_dim = DIFF_HEAD_DIM // 2
    inv_freq = ROPE_THETA ** (-jnp.arange(0, axis_dim, 2, dtype=jnp.float32) / axis_dim)
    ang = jnp.concatenate([row[:, None] * inv_freq, col[:, None] * inv_freq], axis=-1)
    return jnp.cos(ang), jnp.sin(ang)


def apply_rope(x, cos, sin):
    x1, x2 = jnp.split(x, 2, axis=-1)
    c = cos[:, None, :].astype(x.dtype)
    s = sin[:, None, :].astype(x.dtype)
    return jnp.concatenate([x1 * c - x2 * s, x1 * s + x2 * c], axis=-1)


def rope_heads(q, cos, sin):
    B, S, H, two, d = q.shape
    return apply_rope(q.reshape(B, S, H * two, d), cos, sin).reshape(q.shape)


def diff_attend(q, k, v, lam):
    s = jnp.einsum('bqhcd,bkhcd->bhcqk', q, k, preferred_element_type=jnp.float32) * ATTN_SCALE
    p = jax.nn.softmax(s, axis=-1)
    a = p[:, :, 0] - lam * p[:, :, 1]
    return jnp.einsum('bhqk,bkhe->bqhe', a.astype(v.dtype), v)


def fourier_groups(f, w):
    B, T, _ = f.shape
    fg = f.reshape(B, T, FOURIER_GROUPS, FOURIER_GROUP_DIM).astype(jnp.float32)
    z = jnp.fft.fft2(fg, axes=(1, 3), norm='ortho').real.astype(f.dtype)
    return jnp.einsum('btgc,gce->btge', z, w).reshape(B, T, FOURIER_WIDTH)


def split_proj(u):
    B, T, _ = u.shape
    q, k, v, f = jnp.split(u, [DIFF_WIDTH, 2 * DIFF_WIDTH, 3 * DIFF_WIDTH], axis=-1)
    return (q.reshape(B, T, DIFF_HEADS, 2, DIFF_HEAD_DIM),
            k.reshape(B, T, DIFF_HEADS, 2, DIFF_HEAD_DIM),
            v.reshape(B, T, DIFF_HEADS, DIFF_V_DIM),
            f)


def merge_heads(o, f, subln_g, fourier_w, w_out, lambda_init):
    B, T = o.shape[:2]
    o = (rmsnorm(o, subln_g) * (1.0 - lambda_init)).reshape(B, T, DIFF_WIDTH)
    return jnp.concatenate([o, fourier_groups(f, fourier_w)], axis=-1) @ w_out


def diff_fourier_mixer(h_l, h_c, cos, sin, w_in, lq1, lk1, lq2, lk2, subln_g, fourier_w, w_out,
                       lambda_init, ctx_update):
    B, S, _ = h_l.shape
    q_l, k_l, v_l, f_l = split_proj(h_l @ w_in)
    q_c, k_c, v_c, f_c = split_proj(h_c @ w_in)
    q_l = rope_heads(q_l, cos, sin)
    k_l = rope_heads(k_l, cos, sin)
    lam = (jnp.exp(jnp.sum(lq1.astype(jnp.float32) * lk1.astype(jnp.float32)))
           - jnp.exp(jnp.sum(lq2.astype(jnp.float32) * lk2.astype(jnp.float32))) + lambda_init)
    k_all = jnp.concatenate([k_c, k_l], axis=1)
    v_all = jnp.concatenate([v_c, v_l], axis=1)
    n_blk = S // Q_BLOCK
    q_blocks = jnp.moveaxis(q_l.reshape(B, n_blk, Q_BLOCK, DIFF_HEADS, 2, DIFF_HEAD_DIM), 1, 0)
    o = lax.map(lambda qb: diff_attend(qb, k_all, v_all, lam), q_blocks)
    o_l = jnp.moveaxis(o, 0, 1).reshape(B, S, DIFF_HEADS, DIFF_V_DIM)
    y_l = merge_heads(o_l, f_l, subln_g, fourier_w, w_out, lambda_init)
    y_c = None
    if ctx_update:
        y_c = merge_heads(diff_attend(q_c, k_c, v_c, lam), f_c, subln_g, fourier_w, w_out, lambda_init)
    return y_l, y_c


def depthwise_conv(u, w):
    return lax.conv_general_dilated(
        u, w[:, None, :].astype(u.dtype), window_strides=(1,),
        padding=[(CONV_WIDTH // 2, CONV_WIDTH // 2)],
        dimension_numbers=('NWC', 'WIO', 'NWC'), feature_group_count=u.shape[-1])


def conformer_conv(h, pw1_w, pw1_b, dw_w, dw_b, ln_g, ln_b, pw2_w, pw2_b):
    a, g = jnp.split(h @ pw1_w + pw1_b, 2, axis=-1)
    u = a * jax.nn.sigmoid(g)
    u = depthwise_conv(u, dw_w) + dw_b
    u = jax.nn.silu(layernorm(u, ln_g, ln_b))
    return u @ pw2_w + pw2_b


def swiglu(h, w1, w3, w2):
    return (jax.nn.silu(h @ w1) * (h @ w3)) @ w2


def setup_inputs(seed: int = 0) -> dict:
    key = jax.random.key(seed)
    ks = iter(jax.random.split(key, 32))
    n_even = (DEPTH + 1) // 2
    n_odd = DEPTH // 2
    f32 = jnp.float32

    def nrm(shape):
        return jax.random.normal(next(ks), shape, f32)

    def w(shape, fan_in):
        return nrm(shape) * fan_in ** -0.5

    def gain(shape):
        return 1.0 + 0.05 * nrm(shape)

    def bias(shape):
        return 0.02 * nrm(shape)

    D = D_MODEL
    return {
        'x': nrm((BATCH, SEQ, D)),
        'c': nrm((BATCH, D)),
        'ctx': nrm((BATCH, CTX_LEN, D)),
        'c_ctx': nrm((D,)),
        'ada_w': w((DEPTH, D, 6 * D), D),
        'ada_b': bias((DEPTH, 6 * D)),
        'mix_norm_g': gain((DEPTH, D)),
        'ffn_norm_g': gain((DEPTH, D)),
        'ffn_w1': w((DEPTH, D, D_FF), D),
        'ffn_w3': w((DEPTH, D, D_FF), D),
        'ffn_w2': w((DEPTH, D_FF, D), D_FF),
        'ev_w_in': w((n_even, D, IN_WIDTH_EVEN), D),
        'ev_lambda_q1': LAMBDA_STD * nrm((n_even, DIFF_HEAD_DIM)),
        'ev_lambda_k1': LAMBDA_STD * nrm((n_even, DIFF_HEAD_DIM)),
        'ev_lambda_q2': LAMBDA_STD * nrm((n_even, DIFF_HEAD_DIM)),
        'ev_lambda_k2': LAMBDA_STD * nrm((n_even, DIFF_HEAD_DIM)),
        'ev_subln_g': gain((n_even, DIFF_V_DIM)),
        'ev_fourier_w': w((n_even, FOURIER_GROUPS, FOURIER_GROUP_DIM, FOURIER_GROUP_DIM), FOURIER_GROUP_DIM),
        'ev_w_out': w((n_even, D, D), D),
        'od_pw1_w': w((n_odd, D, 2 * D), D),
        'od_pw1_b': bias((n_odd, 2 * D)),
        'od_dw_w': w((n_odd, CONV_WIDTH, D), CONV_WIDTH),
        'od_dw_b': bias((n_odd, D)),
        'od_ln_g': gain((n_odd, D)),
        'od_ln_b': bias((n_odd, D)),
        'od_pw2_w': w((n_odd, D, D), D),
        'od_pw2_b': bias((n_odd, D)),
        'final_g': gain((D,)),
    }


def reference(x, c, ctx, c_ctx, ada_w, ada_b, mix_norm_g, ffn_norm_g, ffn_w1, ffn_w3, ffn_w2,
              ev_w_in, ev_lambda_q1, ev_lambda_k1, ev_lambda_q2, ev_lambda_k2, ev_subln_g,
              ev_fourier_w, ev_w_out, od_pw1_w, od_pw1_b, od_dw_w, od_dw_b, od_ln_g, od_ln_b,
              od_pw2_w, od_pw2_b, final_g):
    S = x.shape[1]
    ROWS = S // GRID_W
    cos, sin = axial_rope_tables(ROWS)
    silu_c = jax.nn.silu(c)
    silu_cc = jax.nn.silu(c_ctx)
    xc = ctx
    for i in range(DEPTH):
        j = i // 2
        even = i % 2 == 0
        ctx_update = any(k % 2 == 0 for k in range(i + 1, DEPTH))
        read_ctx = even or ctx_update
        mod_l = jnp.split((silu_c @ ada_w[i] + ada_b[i])[:, None, :], 6, axis=-1)
        h_l = modulate(rmsnorm(x, mix_norm_g[i]), mod_l[0], mod_l[1])
        mod_c = None
        h_c = None
        if read_ctx:
            mod_c = jnp.split((silu_cc @ ada_w[i] + ada_b[i])[None, None, :], 6, axis=-1)
            h_c = modulate(rmsnorm(xc, mix_norm_g[i]), mod_c[0], mod_c[1])
        if even:
            lambda_init = 0.8 - 0.6 * math.exp(-0.3 * i)
            y_l, y_c = diff_fourier_mixer(h_l, h_c, cos, sin, ev_w_in[j], ev_lambda_q1[j], ev_lambda_k1[j],
                                          ev_lambda_q2[j], ev_lambda_k2[j], ev_subln_g[j], ev_fourier_w[j],
                                          ev_w_out[j], lambda_init, ctx_update)
        else:
            conv_args = (od_pw1_w[j], od_pw1_b[j], od_dw_w[j], od_dw_b[j], od_ln_g[j], od_ln_b[j],
                         od_pw2_w[j], od_pw2_b[j])
            y_l = conformer_conv(h_l, *conv_args)
            y_c = conformer_conv(h_c, *conv_args) if ctx_update else None
        x = x + mod_l[2] * y_l
        x = x + mod_l[5] * swiglu(modulate(rmsnorm(x, ffn_norm_g[i]), mod_l[3], mod_l[4]),
                                  ffn_w1[i], ffn_w3[i], ffn_w2[i])
        if ctx_update:
            xc = xc + mod_c[2] * y_c
            xc = xc + mod_c[5] * swiglu(modulate(rmsnorm(xc, ffn_norm_g[i]), mod_c[3], mod_c[4]),
                                        ffn_w1[i], ffn_w3[i], ffn_w2[i])
    return rmsnorm(x, final_g)
```

```python
import numpy as np
import ml_dtypes
from contextlib import ExitStack
import concourse.bass as bass
import concourse.mybir as mybir
from concourse.bass_utils import run_bass_kernel_spmd

F32 = mybir.dt.float32
BF16 = mybir.dt.bfloat16
AF = mybir.ActivationFunctionType
ALU = mybir.AluOpType

D = 1024
SEQ = 4096
CTX = 256
NT = SEQ + CTX
OWN = 2048
HALO = 16
NQ = OWN + 2 * HALO
DFF = 2816
NH = 6
EPS = 1e-6
LAMBDA_INIT0 = 0.8 - 0.6 * 1.0

VC = {}
_off = 0
for _n, _w in [("ada_b0", 48), ("ada_b1", 48), ("mixg0", 8), ("mixg1", 8), ("ffng0", 8), ("ffng1", 8),
               ("finalg", 8), ("pw1b", 16), ("dwb", 8), ("lng", 8), ("lnb", 8), ("pw2b", 8),
               ("sublng", 1), ("ml", 1), ("mr", 1), ("dww", 248), ("lam", 256)]:
    VC[_n] = (_off, _w)
    _off += _w
NV = _off


class Buf:
    __slots__ = ("name", "w", "r")

    def __init__(self, name):
        self.name = name
        self.w = None
        self.r = []


class Op:
    __slots__ = ("eng", "fn", "deps", "idx", "needed", "val", "dma_sem", "dma_val", "waits")


class _Rec:
    def __init__(self):
        self.call = None

    def __getattr__(self, name):
        def f(*a, **k):
            self.call = (name, a, k)
            return None
        return f


class Prog:
    ENG = ("pe", "act", "dve", "pool", "sp")
    BLK = {"pe": "tensor", "act": "scalar", "dve": "vector", "pool": "gpsimd", "sp": "sync"}

    def __init__(self, nc):
        self.nc = nc
        self.q = {e: [] for e in self.ENG}
        self.dma_cnt = {}
        self.dma_last = {}

    def _mk(self, eng, fn, reads, writes):
        o = Op()
        o.eng = eng
        rec = _Rec()
        fn(rec)
        assert rec.call is not None
        o.fn = rec.call
        o.dma_sem = None
        o.dma_val = 0
        o.needed = False
        o.val = 0
        deps = []
        for b in reads:
            if b.w is not None:
                deps.append(b.w)
        for b in writes:
            if b.w is not None:
                deps.append(b.w)
            for r in b.r:
                if r.dma_sem is None and r.eng == eng:
                    continue
                deps.append(r)
        o.deps = deps
        o.idx = len(self.q[eng])
        self.q[eng].append(o)
        for b in reads:
            b.r.append(o)
        for b in writes:
            b.w = o
            b.r = []
        return o

    def op(self, eng, fn, reads=(), writes=()):
        return self._mk(eng, fn, reads, writes)

    def dma(self, eng, out_ap, in_ap, sem, reads=(), writes=()):
        o = self._mk(eng, lambda e: e.dma_start(out=out_ap, in_=in_ap), reads, writes)
        self.dma_cnt[sem] = self.dma_cnt.get(sem, 0) + 1
        o.dma_sem = sem
        o.dma_val = 16 * self.dma_cnt[sem]
        self.dma_last[sem] = o
        return o

    def barrier(self):
        o = Op()
        o.eng = "sp"
        o.fn = None
        o.dma_sem = None
        o.dma_val = 0
        o.needed = False
        o.val = 0
        o.deps = [self.q[e][-1] for e in self.ENG if e != "sp" and self.q[e]] + list(self.dma_last.values())
        o.idx = len(self.q["sp"])
        self.q["sp"].append(o)
        for e in self.ENG:
            if e == "sp":
                continue
            w = Op()
            w.eng = e
            w.fn = None
            w.dma_sem = None
            w.dma_val = 0
            w.needed = False
            w.val = 0
            w.deps = [o]
            w.idx = len(self.q[e])
            self.q[e].append(w)

    def emit(self):
        nc = self.nc
        for eng in self.ENG:
            seen = {}
            for o in self.q[eng]:
                waits = []
                for d in o.deps:
                    if d.dma_sem is not None:
                        key = ("d", d.dma_sem)
                        v = d.dma_val
                    else:
                        if d.eng == eng and eng == "pe":
                            continue
                        key = ("e", d.eng)
                        v = d.idx
                    if seen.get(key, -1) >= v:
                        continue
                    seen[key] = v
                    waits.append(d)
                    if d.dma_sem is None:
                        d.needed = True
                o.waits = waits
        for eng in self.ENG:
            c = 0
            for o in self.q[eng]:
                if o.needed:
                    c += 1
                    o.val = c
        with ExitStack() as st:
            esem = {e: st.enter_context(nc.semaphore("e_" + e)) for e in self.ENG}
            dsem = {s: st.enter_context(nc.semaphore("d_" + s)) for s in self.dma_cnt}
            block = st.enter_context(nc.Block())
            for eng in self.ENG:
                def body(e, eng=eng):
                    for o in self.q[eng]:
                        for d in o.waits:
                            if d.dma_sem is not None:
                                e.wait_ge(dsem[d.dma_sem], d.dma_val)
                            else:
                                e.wait_ge(esem[d.eng], d.val)
                        ins = None
                        if o.fn is not None:
                            m_, a_, k_ = o.fn
                            ins = getattr(e, m_)(*a_, **k_)
                        if o.dma_sem is not None:
                            ins.then_inc(dsem[o.dma_sem], 16)
                        if o.needed:
                            if ins is not None and o.dma_sem is None:
                                ins.then_inc(esem[eng], 1)
                            else:
                                e.sem_inc(esem[eng], 1)
                getattr(block, self.BLK[eng])(body)


def build_program(upto=99, debug=False):
    nc = bass.Bass("TRN2", target_bir_lowering=False)
    P = Prog(nc)

    def din(name, shape, dt=F32):
        return nc.dram_tensor(name, list(shape), dt, kind="ExternalInput").ap()

    xT = din("xT", [128, 8, NT])
    cvec = din("cvec", [128, 16])
    vecs = din("vecs", [128, NV])
    ada_w = din("ada_w", [2, D, 6 * D])
    w_ext = din("w_ext", [D, 2304])
    permf_d = din("permf", [128, 128])
    w_fT = din("w_fT", [256, D])
    fw_bd = din("fw_bd", [256, 256])
    ccss = din("ccss", [256, 512], BF16)
    identb = din("identb", [128, 128], BF16)
    ropeK = din("ropeK", [128, 2, NT])
    ropeQ = din("ropeQ", [128, 2, NQ])
    dftC = din("dftC", [32, 128, NQ], BF16)
    dftS = din("dftS", [32, 128, NQ], BF16)
    w_out = din("w_out", [D, D])
    ffn_w1 = din("ffn_w1", [2, D, DFF])
    ffn_w3 = din("ffn_w3", [2, D, DFF])
    ffn_w2 = din("ffn_w2", [2, DFF, D])
    pw1_w = din("pw1_w", [D, 2 * D])
    pw2_w = din("pw2_w", [D, D])
    outT = nc.dram_tensor("outT", [128, 8, OWN], F32, kind="ExternalOutput").ap()

    skind = "ExternalOutput" if debug else "Internal"
    KT_d = nc.dram_tensor("KT_d", [NH, 128, NT], BF16, kind=skind).ap()
    V_d = nc.dram_tensor("V_d", [34, 128, 768], BF16, kind=skind).ap()
    G_d = nc.dram_tensor("G_d", [32, 128, 512], BF16, kind=skind).ap()
    QT_d = nc.dram_tensor("QT_d", [NH, 128, NQ], BF16, kind=skind).ap()
    if debug:
        dbgX = nc.dram_tensor("dbgX", [128, 8, NQ], F32, kind="ExternalOutput").ap()
        dbgO = nc.dram_tensor("dbgO", [128, NH, NQ], F32, kind="ExternalOutput").ap()
        dbgM = nc.dram_tensor("dbgM", [128, 200], F32, kind="ExternalOutput").ap()
        dbgU = nc.dram_tensor("dbgU", [128, 8, NQ], BF16, kind="ExternalOutput").ap()

    gs = ExitStack()

    uid = [0]

    def sb(name, shape, dt, stack=gs):
        uid[0] += 1
        return stack.enter_context(nc.sbuf_tensor("%s_%d" % (name, uid[0]), list(shape), dt))

    ps = gs.enter_context(nc.psum_tensor("ps", [128, 8, 512], F32))
    PSB = [Buf("ps%d" % i) for i in range(8)]

    vec = sb("vec", [128, NV], F32)
    b_vec = Buf("vec")
    ones = sb("ones", [128, 128], BF16)
    b_ones = Buf("ones")
    ident = sb("ident", [128, 128], BF16)
    b_ident = Buf("ident")
    sel = sb("sel", [64, 2, 128], F32)
    b_sel = Buf("sel")
    epsb = sb("epsb", [128, 1], F32)
    b_eps = Buf("epsb")
    mods = sb("mods", [128, 2, 2, 48], F32)
    b_mods = Buf("mods")
    drv = sb("drv", [128, 2, 7, 8], F32)
    b_drv = Buf("drv")
    b_drv2 = Buf("drv2")
    drvc = sb("drvc", [128, 2, 8], F32)
    b_drvc = Buf("drvc")
    lamt = sb("lamt", [128, 4], F32)
    b_lam = Buf("lamt")
    b_X = [Buf("X%d" % i) for i in range(5)]

    def v(name, a=0, b=None):
        o, w = VC[name]
        if b is None:
            b = w
        return vec[:, o + a:o + b]

    QB = [(i * 512, 512) for i in range(4)] + [(2048, 32)]

    P.dma("sp", vec[:, :], vecs[:, :], "vec", writes=[b_vec])
    P.dma("sp", ident[:, :], identb[:, :], "ident", writes=[b_ident])
    P.op("dve", lambda e: e.memset(ones[:, :], 1.0), writes=[b_ones])
    P.op("dve", lambda e: e.memset(epsb[:, :], EPS), writes=[b_eps])
    P.op("dve", lambda e: e.memset(sel[:, :, :], 0.0), writes=[b_sel])
    P.op("dve", lambda e: e.memset(sel[0:32, 0, :], 1.0 / 32.0), writes=[b_sel])
    P.op("dve", lambda e: e.memset(sel[32:64, 1, :], 1.0 / 32.0), writes=[b_sel])

    ws = ExitStack()
    scb = sb("scb", [128, 8, 2], BF16, ws)
    b_scb = Buf("scb")
    wa = [sb("wa%d" % i, [128, 8, 512], BF16, ws) for i in range(3)]
    b_wa = [Buf("wa%d" % i) for i in range(3)]
    ada_v = ada_w.rearrange("l (c p) n -> l p c n", p=128)
    ada_state = {"next": 0, "dma": 0}
    b_adaps = [Buf("adaps%d" % i) for i in range(3)]

    def ada_finish(li):
        mg = "mixg%d" % li
        fg = "ffng%d" % li
        P.op("dve", lambda e: e.scalar_tensor_tensor(
            out=drv[:, li, 3, :], in0=mods[:, li, 0, 32:40], scalar=1.0, in1=v(fg), op0=ALU.add, op1=ALU.mult),
            reads=[b_mods, b_vec], writes=[b_drv2])
        P.op("dve", lambda e: e.tensor_copy(out=drv[:, li, 2, :], in_=mods[:, li, 0, 16:24]), reads=[b_mods], writes=[b_drv2])
        P.op("dve", lambda e: e.tensor_copy(out=drv[:, li, 4, :], in_=mods[:, li, 0, 24:32]), reads=[b_mods], writes=[b_drv2])
        P.op("dve", lambda e: e.tensor_copy(out=drv[:, li, 5, :], in_=mods[:, li, 0, 40:48]), reads=[b_mods], writes=[b_drv2])
        P.op("dve", lambda e: e.tensor_tensor(out=drv[:, li, 6, :], in0=mods[:, li, 0, 16:24], in1=v("pw2b"), op=ALU.mult),
             reads=[b_mods, b_vec], writes=[b_drv2])

    def ada_first(li):
        mg = "mixg%d" % li
        P.op("dve", lambda e: e.scalar_tensor_tensor(
            out=drv[:, li, 0, :], in0=mods[:, li, 0, 8:16], scalar=1.0, in1=v(mg), op0=ALU.add, op1=ALU.mult),
            reads=[b_mods, b_vec], writes=[b_drv if li == 0 else b_drv2])
        P.op("dve", lambda e: e.tensor_copy(out=drv[:, li, 1, :], in_=mods[:, li, 0, 0:8]), reads=[b_mods], writes=[b_drv if li == 0 else b_drv2])
        if li == 0:
            P.op("dve", lambda e: e.scalar_tensor_tensor(
                out=drvc[:, 0, :], in0=mods[:, 0, 1, 8:16], scalar=1.0, in1=v("mixg0"), op0=ALU.add, op1=ALU.mult),
                reads=[b_mods, b_vec], writes=[b_drvc])
            P.op("dve", lambda e: e.tensor_copy(out=drvc[:, 1, :], in_=mods[:, 0, 1, 0:8]), reads=[b_mods], writes=[b_drvc])

    def ada_dma(k):
        for _ in range(k):
            it = ada_state["dma"]
            if it >= 24:
                return
            ada_state["dma"] = it + 1
            li, pc = it // 12, it % 12
            s_ = it % 3
            P.dma("pool", wa[s_][:, :, :], ada_v[li, :, :, pc * 512:(pc + 1) * 512], "wa%d" % s_, writes=[b_wa[s_]])

    def ada_jobs(k, psb=0):
        for _ in range(k):
            it = ada_state["next"]
            if it >= 24:
                return
            if ada_state["dma"] <= it:
                ada_dma(1)
            ada_state["next"] = it + 1
            li, pc = it // 12, it % 12
            s_ = it % 3
            for g in range(4):
                col = g * 2
                for k_ in range(8):
                    P.op("pe", lambda e: e.matmul(ps[:, psb, col:col + 2], lhsT=wa[s_][:, k_, g * 128:(g + 1) * 128], rhs=scb[:, k_, :],
                                                  start=(k_ == 0), stop=(k_ == 7)), reads=[b_wa[s_], b_scb], writes=[PSB[psb]])
            psv = ps[:, psb, 0:8].rearrange("p (g t) -> p g t", t=2)
            ao = VC["ada_b%d" % li][0]
            for t in range(2):
                P.op("dve", lambda e: e.tensor_tensor(out=mods[:, li, t, pc * 4:(pc + 1) * 4], in0=psv[:, :, t], in1=vec[:, ao + pc * 4:ao + (pc + 1) * 4], op=ALU.add),
                     reads=[PSB[psb], b_vec], writes=[b_mods])
            if pc == 3:
                ada_first(li)
            if pc == 11:
                ada_finish(li)

    with ExitStack() as ls:
        cv = sb("cv", [128, 16], F32, ls)
        b_cv = Buf("cv")
        P.dma("sp", cv[:, :], cvec[:, :], "cv", writes=[b_cv])
        sg = sb("sg", [128, 16], F32, ls)
        b_sg = Buf("sg")
        P.op("act", lambda e: e.activation(out=sg[:, :], in_=cv[:, :], func=AF.Sigmoid), reads=[b_cv], writes=[b_sg])
        P.op("dve", lambda e: e.tensor_tensor(out=scb[:, :, :].rearrange("p c k -> p (c k)"), in0=cv[:, :], in1=sg[:, :], op=ALU.mult),
             reads=[b_cv, b_sg], writes=[b_scb])
        lt = sb("lt", [128, 128], F32, ls)
        b_lt = Buf("lt")
        lo = VC["lam"][0]
        P.op("dve", lambda e: e.tensor_tensor(out=lt[:, 0:64], in0=vec[:, lo:lo + 64], in1=vec[:, lo + 64:lo + 128], op=ALU.mult),
             reads=[b_vec], writes=[b_lt])
        P.op("dve", lambda e: e.tensor_tensor(out=lt[:, 64:128], in0=vec[:, lo + 128:lo + 192], in1=vec[:, lo + 192:lo + 256], op=ALU.mult),
             reads=[b_vec], writes=[b_lt])
        ls2 = sb("ls2", [128, 2], F32, ls)
        b_ls2 = Buf("ls2")
        P.op("dve", lambda e: e.tensor_reduce(out=ls2[:, :], in_=lt[:, :].rearrange("p (a b) -> p a b", a=2),
                                              axis=mybir.AxisListType.X, op=ALU.add), reads=[b_lt], writes=[b_ls2])
        P.op("act", lambda e: e.activation(out=lamt[:, 0:2], in_=ls2[:, :], func=AF.Exp), reads=[b_ls2], writes=[b_lam])
        P.op("dve", lambda e: e.scalar_tensor_tensor(out=lamt[:, 2:3], in0=lamt[:, 1:2], scalar=-LAMBDA_INIT0, in1=lamt[:, 0:1],
                                                     op0=ALU.add, op1=ALU.subtract), reads=[b_lam], writes=[b_lam])
        P.op("dve", lambda e: e.tensor_scalar(out=lamt[:, 3:4], in0=v("sublng"), scalar1=1.0 - LAMBDA_INIT0, scalar2=None, op0=ALU.mult),
             reads=[b_vec], writes=[b_lam])
        ada_dma(3)
        ada_jobs(1)
        ada_dma(1)
        ada_jobs(3)
        if debug:
            ada_jobs(24)
            P.dma("sp", dbgM[:, 0:96], mods[:, 0, :, :].rearrange("p a b -> p (a b)"), "dbg", reads=[b_mods])
            P.dma("sp", dbgM[:, 96:192], mods[:, 1, :, :].rearrange("p a b -> p (a b)"), "dbg", reads=[b_mods])
            P.dma("sp", dbgM[:, 192:196], lamt[:, :], "dbg", reads=[b_lam])
        P.barrier()

    def rms_mod(src3, n, src_bufs, A, B, AB_bufs, hb3, hb_buf, tmp3, tmp_buf, sq3, sq_buf, rstd, rstd_buf, psb, out_f32=False):
        rms_a(src3, src_bufs, sq3, sq_buf)
        rms_b(src3, n, src_bufs, A, B, AB_bufs, hb3, hb_buf, tmp3, tmp_buf, sq3, sq_buf, rstd, rstd_buf, psb)

    def rms_a(src3, src_bufs, sq3, sq_buf):
        P.op("act", lambda e: e.activation(out=sq3, in_=src3, func=AF.Square), reads=src_bufs, writes=[sq_buf])

    def rms_b(src3, n, src_bufs, A, B, AB_bufs, hb3, hb_buf, tmp3, tmp_buf, sq3, sq_buf, rstd, rstd_buf, psb):
        rms_b1(src3, n, src_bufs, tmp3, tmp_buf, sq3, sq_buf, rstd, rstd_buf, psb)
        rms_b2(A, B, AB_bufs, hb3, hb_buf, tmp3, tmp_buf)

    def rms_b1(src3, n, src_bufs, tmp3, tmp_buf, sq3, sq_buf, rstd, rstd_buf, psb):
        for c in range(8):
            P.op("pe", lambda e, c=c: e.matmul(ps[:, psb, 0:n], lhsT=ones[:, :], rhs=sq3[:, c, :], start=(c == 0), stop=(c == 7)),
                 reads=[b_ones, sq_buf], writes=[PSB[psb]])
        P.op("act", lambda e: e.activation(out=rstd, in_=ps[:, psb, 0:n], func=AF.Ln, bias=epsb[:, 0:1], scale=1.0 / D),
             reads=[PSB[psb], b_eps], writes=[rstd_buf])
        P.op("act", lambda e: e.activation(out=rstd, in_=rstd, func=AF.Exp, scale=-0.5), reads=[rstd_buf], writes=[rstd_buf])
        P.op("dve", lambda e: e.tensor_tensor(out=tmp3, in0=src3, in1=rstd.unsqueeze(1).to_broadcast([128, 8, n]), op=ALU.mult),
             reads=list(src_bufs) + [rstd_buf], writes=[tmp_buf])

    def rms_b2(A, B, AB_bufs, hb3, hb_buf, tmp3, tmp_buf):
        for c in range(8):
            if B is None:
                P.op("dve", lambda e, c=c: e.tensor_scalar(out=hb3[:, c, :], in0=tmp3[:, c, :], scalar1=A[:, c:c + 1], scalar2=None, op0=ALU.mult),
                     reads=[tmp_buf] + AB_bufs, writes=[hb_buf])
            elif c % 2 == 0:
                P.op("act", lambda e, c=c: e.activation(out=hb3[:, c, :], in_=tmp3[:, c, :], func=AF.Identity,
                                                        bias=B[:, c:c + 1], scale=A[:, c:c + 1]),
                     reads=[tmp_buf] + AB_bufs, writes=[hb_buf])
            else:
                P.op("dve", lambda e, c=c: e.tensor_scalar(out=hb3[:, c, :], in0=tmp3[:, c, :], scalar1=A[:, c:c + 1], scalar2=B[:, c:c + 1],
                                                           op0=ALU.mult, op1=ALU.add),
                     reads=[tmp_buf] + AB_bufs, writes=[hb_buf])

    if upto < 1:
        P.emit()
        ws.close()
        gs.close()
        return nc

    with ExitStack() as ls:
        wq = sb("wq", [128, 8, 768], BF16, ls)
        wk = sb("wk", [128, 8, 768], BF16, ls)
        permf = sb("permf", [128, 128], F32, ls)
        b_permf = Buf("permf")
        P.dma("sp", permf[:, :], permf_d[:, :], "permf", writes=[b_permf])
        kf = [sb("kf%d" % i, [128, 512], F32, ls) for i in range(2)]
        b_kf = [Buf("kf%d" % i) for i in range(2)]
        wvg = sb("wvg", [128, 8, 1280], BF16, ls)
        b_wq, b_wk, b_wvg = Buf("wq"), Buf("wk"), Buf("wvg")
        wv_ = w_ext.rearrange("(c p) n -> p c n", p=128)
        if True:
            l2 = ls
            fwb = sb("fwb", [128, 2, 256], BF16, l2)
            ccb = sb("ccb", [128, 2, 512], BF16, l2)
            wft = sb("wft", [128, 2, D], BF16, l2)
            m12 = sb("m12", [128, 2, 512], BF16, l2)
            b_fwb, b_ccb, b_wft, b_m12 = Buf("fwb"), Buf("ccb"), Buf("wft"), Buf("m12")
            P.dma("pool", fwb[:, :, :], fw_bd.rearrange("(c p) n -> p c n", p=128), "fwb", writes=[b_fwb])
            P.dma("sp", ccb[:, :, :], ccss.rearrange("(c p) n -> p c n", p=128), "ccb", writes=[b_ccb])
            P.dma("pool", wft[:, :, :], w_fT.rearrange("(c p) n -> p c n", p=128), "wft", writes=[b_wft])
            P.dma("pool", wk[:, :, :], wv_[:, :, 768:1536], "wk", writes=[b_wk])
            P.dma("pool", wvg[:, :, 0:768], wv_[:, :, 1536:2304], "wvg", writes=[b_wvg])
            P.dma("pool", wq[:, :, :], wv_[:, :, 0:768], "wq", writes=[b_wq])
            if not debug:
                ada_dma(3)
            for cc in range(2):
                for half in range(2):
                    for lc in range(2):
                        P.op("pe", lambda e, cc=cc, half=half, lc=lc: e.matmul(
                            ps[:, 1 + cc, half * 256:(half + 1) * 256],
                            lhsT=ccb[:, lc, half * 256 + cc * 128: half * 256 + (cc + 1) * 128], rhs=fwb[:, lc, :],
                            start=(lc == 0), stop=(lc == 1)), reads=[b_ccb, b_fwb], writes=[PSB[1 + cc]])
                P.op("act", lambda e, cc=cc: e.activation(out=m12[:, cc, :], in_=ps[:, 1 + cc, :], func=AF.Copy),
                     reads=[PSB[1 + cc]], writes=[b_m12])
            for dc in range(8):
                bk = 3 + dc % 2
                for cc in range(2):
                    P.op("pe", lambda e, dc=dc, cc=cc, bk=bk: e.matmul(
                        ps[:, bk, :], lhsT=wft[:, cc, dc * 128:(dc + 1) * 128], rhs=m12[:, cc, :],
                        start=(cc == 0), stop=(cc == 1)), reads=[b_wft, b_m12], writes=[PSB[bk]])
                P.op("dve", lambda e, dc=dc, bk=bk: e.tensor_copy(out=wvg[:, dc, 768:1280], in_=ps[:, bk, :]),
                     reads=[PSB[bk]], writes=[b_wvg])

        xb = [sb("xb%d" % i, [128, 8, 512], F32, ls) for i in range(2)]
        b_xb = [Buf("xb%d" % i) for i in range(2)]
        sq = sb("sq", [128, 8, 512], BF16, ls)
        b_sq = Buf("sq")
        hb = [sb("hb%d" % i, [128, 8, 512], BF16, ls) for i in range(2)]
        b_hb = [Buf("hb%d" % i) for i in range(2)]
        rstd = sb("rstd", [128, 512], F32, ls)
        b_rstd = Buf("rstd")
        rk = [sb("rk%d" % i, [128, 2, 512], F32, ls) for i in range(2)]
        b_rk = [Buf("rk%d" % i) for i in range(2)]
        rq = sb("rq", [128, 2, NQ], F32, ls)
        b_rq = Buf("rq")
        t1 = [sb("t1_%d" % i, [128, 512], F32, ls) for i in range(2)]
        t2 = [sb("t2_%d" % i, [128, 512], F32, ls) for i in range(2)]
        b_t1 = [Buf("t1_%d" % i) for i in range(2)]
        b_t2 = [Buf("t2_%d" % i) for i in range(2)]
        kst = [sb("kst%d" % i, [128, NH, 512], BF16, ls) for i in range(2)]
        b_kst = [Buf("kst%d" % i) for i in range(2)]
        qst = [sb("qst0", [128, NH, 512], BF16, ls)] * 2
        b_qst = [Buf("qst0")] * 2
        vst = [sb("vst%d" % i, [128, 1280], BF16, ls) for i in range(2)]
        b_vst = [Buf("vst%d" % i) for i in range(2)]
        P.dma("sp", rq[:, :, :], ropeQ[:, :, :], "rq", writes=[b_rq])

        rot = [0]

        def rope_main(w, wbuf, hbt, hbbuf, hcols, n, h):
            r = rot[0] % 2
            rot[0] += 1
            ba = 1 + 2 * r
            for k in range(8):
                P.op("pe", lambda e, k=k: e.matmul(ps[:, ba, 0:n], lhsT=w[:, k, h * 128:(h + 1) * 128], rhs=hbt[:, k, hcols],
                                                   start=(k == 0), stop=(k == 7)), reads=[wbuf, hbbuf], writes=[PSB[ba]])
            P.op("act", lambda e: e.activation(out=kf[r][:, 0:n], in_=ps[:, ba, 0:n], func=AF.Copy), reads=[PSB[ba]], writes=[b_kf[r]])
            return r

        def rope_rest(r, n, cos_ap, sin_ap, tbufs, dst, dst_buf):
            bb = 2 + 2 * r
            P.op("pe", lambda e: e.matmul(ps[:, bb, 0:n], lhsT=permf[:, :], rhs=kf[r][:, 0:n], start=True, stop=True),
                 reads=[b_permf, b_kf[r]], writes=[PSB[bb]])
            P.op("dve", lambda e: e.tensor_tensor(out=t1[r][:, 0:n], in0=kf[r][:, 0:n], in1=cos_ap, op=ALU.mult),
                 reads=[b_kf[r]] + tbufs, writes=[b_t1[r]])
            P.op("dve", lambda e: e.tensor_tensor(out=t2[r][:, 0:n], in0=ps[:, bb, 0:n], in1=sin_ap, op=ALU.mult),
                 reads=[PSB[bb]] + tbufs, writes=[b_t2[r]])
            P.op("pool", lambda e: e.tensor_tensor(out=dst, in0=t1[r][:, 0:n], in1=t2[r][:, 0:n], op=ALU.add),
                 reads=[b_t1[r], b_t2[r]], writes=[dst_buf])

        class RopePipe:
            def __init__(self):
                self.pending = None

            def push(self, w, wbuf, hbt, hbbuf, hcols, n, cos_ap, sin_ap, tbufs, dst, dst_buf, h):
                r = rope_main(w, wbuf, hbt, hbbuf, hcols, n, h)
                self.flush()
                self.pending = (r, n, cos_ap, sin_ap, tbufs, dst, dst_buf)

            def flush(self):
                if self.pending is not None:
                    rope_rest(*self.pending)
                    self.pending = None

        rpipe = RopePipe()

        def rope_proj(w, wbuf, hbt, hbbuf, hcols, n, cos_ap, sin_ap, tbufs, dst, dst_buf, h):
            rpipe.push(w, wbuf, hbt, hbbuf, hcols, n, cos_ap, sin_ap, tbufs, dst, dst_buf, h)

        nblk = 9

        def blkinfo(blk):
            return blk * 512, (512 if blk < 8 else 256), blk % 2

        def load_norm(blk):
            t0, n, s = blkinfo(blk)
            P.dma("sp", xb[s][:, :, 0:n], xT[:, :, t0:t0 + n], "xb%d" % s, writes=[b_xb[s]])
            P.dma("sp", rk[s][:, :, 0:n], ropeK[:, :, t0:t0 + n], "rk%d" % s, writes=[b_rk[s]])
            if blk < 8:
                A, B, ABb = drv[:, 0, 0, :], drv[:, 0, 1, :], [b_drv]
            else:
                A, B, ABb = drvc[:, 0, :], drvc[:, 1, :], [b_drvc]
            rms_mod(xb[s][:, :, 0:n], n, [b_xb[s]], A, B, ABb, hb[s][:, :, 0:n], b_hb[s], xb[s][:, :, 0:n], b_xb[s],
                    sq[:, :, 0:n], b_sq, rstd[:, 0:n], b_rstd, 0)

        load_norm(0)
        for blk in range(nblk):
            t0, n, s = blkinfo(blk)
            ada_jobs(3)
            ada_dma(3)
            if blk + 1 < nblk:
                load_norm(blk + 1)
            ntile = n // 128
            for h in range(NH):
                rope_proj(wk, b_wk, hb[s], b_hb[s], slice(0, n), n, rk[s][:, 0, 0:n], rk[s][:, 1, 0:n], [b_rk[s]],
                          kst[s][:, h, 0:n], b_kst[s], h)
                if h < ntile:
                    j = h
                    vs = (blk * 4 + j) % 2
                    for ci, (c0, c1) in enumerate([(0, 512), (512, 1024), (1024, 1280)]):
                        for k in range(8):
                            P.op("pe", lambda e, k=k, j=j, ci=ci, c0=c0, c1=c1: e.matmul(
                                ps[:, 5 + ci, 0:c1 - c0], lhsT=hb[s][:, k, j * 128:(j + 1) * 128], rhs=wvg[:, k, c0:c1],
                                start=(k == 0), stop=(k == 7)), reads=[b_hb[s], b_wvg], writes=[PSB[5 + ci]])
                    P.op("act", lambda e, vs=vs: e.activation(out=vst[vs][:, 0:1024].rearrange("p (a b) -> p a b", a=2),
                                                              in_=ps[:, 5:7, :], func=AF.Copy),
                         reads=[PSB[5], PSB[6]], writes=[b_vst[vs]])
                    P.op("act", lambda e, vs=vs: e.activation(out=vst[vs][:, 1024:1280], in_=ps[:, 7, 0:256], func=AF.Copy),
                         reads=[PSB[7]], writes=[b_vst[vs]])
                    tix = blk * 4 + j
                    P.dma("pool", V_d[tix, :, :], vst[vs][:, 0:768], "vst%d" % vs, reads=[b_vst[vs]])
                    if blk < 8:
                        P.dma("pool", G_d[tix, :, :], vst[vs][:, 768:1280], "vstg%d" % vs, reads=[b_vst[vs]])
            rpipe.flush()
            P.dma("pool", KT_d[:, :, t0:t0 + n].rearrange("h p t -> p h t"), kst[s][:, :, 0:n], "kst%d" % s, reads=[b_kst[s]])
            if blk < 4:
                for h in range(NH):
                    rope_proj(wq, b_wq, hb[s], b_hb[s], slice(0, 512), 512, rq[:, 0, t0:t0 + 512], rq[:, 1, t0:t0 + 512], [b_rq],
                              qst[s][:, h, :], b_qst[s], h)
                rpipe.flush()
                P.dma("pool", QT_d[:, :, t0:t0 + 512].rearrange("h p t -> p h t"), qst[s][:, :, :], "qst0", reads=[b_qst[s]])
            if blk in (4, 7):
                hc = slice(0, 16) if blk == 4 else slice(496, 512)
                qc = 2064 if blk == 4 else 2048
                for h in range(NH):
                    rope_proj(wq, b_wq, hb[s], b_hb[s], hc, 16, rq[:, 0, qc:qc + 16], rq[:, 1, qc:qc + 16], [b_rq],
                              qst[s][:, h, 0:16], b_qst[s], h)
                rpipe.flush()
                P.dma("pool", QT_d[:, :, qc:qc + 16].rearrange("h p t -> p h t"), qst[s][:, :, 0:16], "qst0", reads=[b_qst[s]])
        P.barrier()

    assert ada_state["next"] == 24
    ws.close()
    if upto < 2:
        P.emit()
        gs.close()
        return nc

    X = sb("X", [128, 8, NQ], F32)
    ms = ExitStack()
    oall = sb("oall", [128, NH, NQ], F32, ms)
    b_oall = [Buf("oall%d" % i) for i in range(5)]
    with ExitStack() as ls:
        ktb = [sb("ktb%d" % i, [128, NT], BF16, ls) for i in range(2)]
        vtb = [sb("vtb%d" % i, [128, 34, 128], BF16, ls) for i in range(2)]
        qtb = [sb("qtb%d" % i, [128, NQ], BF16, ls) for i in range(2)]
        b_ktb = [Buf("ktb%d" % i) for i in range(2)]
        b_vtb = [Buf("vtb%d" % i) for i in range(2)]
        b_qtb = [Buf("qtb%d" % i) for i in range(2)]
        pT = [sb("pT%d" % i, [128, 2, 2, 512], BF16, ls) for i in range(3)]
        b_pT = [[Buf("pT%d_%d" % (i, j)) for j in range(2)] for i in range(3)]
        rs_sb = sb("rs_sb", [64, 512], F32, ls)
        b_rs = Buf("rs_sb")
        bc_sb = sb("bc_sb", [128, 2, 512], F32, ls)
        b_bc = Buf("bc_sb")
        o1 = sb("o1", [128, 512], F32, ls)
        b_o1 = Buf("o1")
        racc = sb("racc", [128, 2, 2, 512], F32, ls)
        b_racc = Buf("racc")
        onesf = sb("onesf", [128, 128], F32, ls)
        b_onesf = Buf("onesf")
        P.op("dve", lambda e: e.memset(onesf[:, :], 1.0), writes=[b_onesf])
        selc = sb("selc", [128, 2, 128], BF16, ls)
        b_selc = Buf("selc")
        P.op("dve", lambda e: e.memset(selc[:, :, :], 0.0), writes=[b_selc])
        for c_ in range(2):
            for j_ in range(2):
                g_ = 2 * j_ + c_
                P.op("dve", lambda e: e.memset(selc[32 * g_:32 * (g_ + 1), c_, :], 1.0 / 32.0), writes=[b_selc])
        rs_pe = sb("rs_pe", [128, 512], BF16, ls)
        b_rspe = Buf("rs_pe")
        raccb = sb("raccb", [128, 2, 2, 512], BF16, ls)
        b_raccb = Buf("raccb")
        o0 = sb("o0", [128, 512], F32, ls)
        b_o0 = Buf("o0")
        PE_PAIRS = (2, 5, 8, 11, 14)
        unit = 0
        QB2 = [(i * 416, 416) for i in range(5)]

        def make_fin(h, qi, q0, n):
            def fin_a():
                P.op("dve", lambda e: e.tensor_copy(out=o0[:, 0:n], in_=ps[:, 4, 0:n]), reads=[PSB[4]], writes=[b_o0])
                P.op("dve", lambda e: e.tensor_copy(out=o1[:, 0:n], in_=ps[:, 5, 0:n]), reads=[PSB[5]], writes=[b_o1])
                P.op("dve", lambda e: e.tensor_copy(out=rs_pe[:, 0:n], in_=ps[:, 7, 0:n]), reads=[PSB[7]], writes=[b_rspe])
                for c in range(2):
                    for j in range(2):
                        P.op("pe", lambda e: e.matmul(ps[:, 6 + c, 0:n], lhsT=ones[:, :], rhs=raccb[:, j, c, 0:n], start=(j == 0), stop=False),
                             reads=[b_ones, b_raccb], writes=[PSB[6 + c]])
                    P.op("pe", lambda e: e.matmul(ps[:, 6 + c, 0:n], lhsT=selc[:, c, :], rhs=rs_pe[:, 0:n], start=False, stop=True),
                         reads=[b_selc, b_rspe], writes=[PSB[6 + c]])

            def fin_b():
                for c in range(2):
                    P.op("act", lambda e: e.activation(out=bc_sb[:, c, 0:n], in_=ps[:, 6 + c, 0:n], func=AF.Ln), reads=[PSB[6 + c]], writes=[b_bc])
                P.op("act", lambda e: e.activation(out=bc_sb[:, :, 0:n], in_=bc_sb[:, :, 0:n], func=AF.Exp, scale=-1.0), reads=[b_bc], writes=[b_bc])
                P.op("dve", lambda e: e.tensor_scalar(out=bc_sb[:, 1, 0:n], in0=bc_sb[:, 1, 0:n], scalar1=lamt[:, 2:3], scalar2=None, op0=ALU.mult),
                     reads=[b_bc, b_lam], writes=[b_bc])
                P.op("pool", lambda e: e.tensor_tensor(out=oall[:, h, q0:q0 + n], in0=o0[:, 0:n], in1=bc_sb[:, 0, 0:n], op=ALU.mult),
                     reads=[b_o0, b_bc], writes=[b_oall[qi]])
                P.op("pool", lambda e: e.tensor_tensor(out=o1[:, 0:n], in0=o1[:, 0:n], in1=bc_sb[:, 1, 0:n], op=ALU.mult),
                     reads=[b_o1, b_bc], writes=[b_o1])
                P.op("pool", lambda e: e.tensor_tensor(out=oall[:, h, q0:q0 + n], in0=oall[:, h, q0:q0 + n], in1=o1[:, 0:n], op=ALU.add),
                     reads=[b_o1, b_oall[qi]], writes=[b_oall[qi]])
            return fin_a, fin_b

        pending_fin = None
        for h in range(NH):
            s = h % 2
            P.dma("sp", ktb[s][:, :], KT_d[h, :, :], "ktb%d" % s, writes=[b_ktb[s]])
            P.dma("sp", vtb[s][:, :, :], V_d[:, :, h * 128:(h + 1) * 128].rearrange("j p e -> p j e"), "vtb%d" % s, writes=[b_vtb[s]])
            P.dma("sp", qtb[s][:, :], QT_d[h, :, :], "qtb%d" % s, writes=[b_qtb[s]])
            for qi, (q0, n) in enumerate(QB2):
                pend = None
                for kp in range(17 + 1):
                    if kp < 17:
                        pb = unit % 3
                        unit += 1

                    def qk_exp(j):
                        kt = 2 * kp + j
                        for c in range(2):
                            P.op("pe", lambda e: e.matmul(
                                ps[:, 2 * j + c, 0:n], lhsT=ktb[s][64 * c:64 * (c + 1), kt * 128:(kt + 1) * 128],
                                rhs=qtb[s][64 * c:64 * (c + 1), q0:q0 + n], start=True, stop=True),
                                reads=[b_ktb[s], b_qtb[s]], writes=[PSB[2 * j + c]])
                        P.op("act", lambda e: e.activation(out=pT[pb][:, j, :, 0:n], in_=ps[:, 2 * j:2 * j + 2, 0:n], func=AF.Exp),
                             reads=[PSB[2 * j], PSB[2 * j + 1]], writes=[b_pT[pb][j]])

                    def av(j):
                        pkp, ppb = pend
                        pkt = 2 * pkp + j
                        for c in range(2):
                            P.op("pe", lambda e: e.matmul(
                                ps[:, 4 + c, 0:n], lhsT=vtb[s][:, pkt, :], rhs=pT[ppb][:, j, c, 0:n],
                                start=(pkt == 0), stop=(pkt == 33)), reads=[b_vtb[s], b_pT[ppb][j]], writes=[PSB[4 + c]])

                    if kp < 17:
                        qk_exp(0)
                    if kp == 0 and pending_fin is not None:
                        pending_fin[0]()
                    if pend is not None:
                        av(0)
                    if kp < 17:
                        qk_exp(1)
                    if kp == 1 and pending_fin is not None:
                        pending_fin[1]()
                        pending_fin = None
                    if pend is not None:
                        av(1)
                        pkp, ppb = pend
                        if pkp in PE_PAIRS:
                            for j in range(2):
                                for c in range(2):
                                    g = 2 * j + c
                                    P.op("pe", lambda e: e.matmul(
                                        ps[32 * g:32 * (g + 1), 7, 0:n], lhsT=ones[:, 0:32], rhs=pT[ppb][:, j, c, 0:n],
                                        start=(pkp == PE_PAIRS[0]), stop=(pkp == PE_PAIRS[-1]), tile_position=(0, 32 * g)),
                                        reads=[b_ones, b_pT[ppb][j]], writes=[PSB[7]])
                        elif pkp == 0:
                            P.op("dve", lambda e: e.tensor_copy(out=racc[:, :, :, 0:n], in_=pT[ppb][:, :, :, 0:n]),
                                 reads=b_pT[ppb], writes=[b_racc])
                        elif pkp == 16:
                            P.op("dve", lambda e: e.tensor_tensor(out=raccb[:, :, :, 0:n], in0=racc[:, :, :, 0:n], in1=pT[ppb][:, :, :, 0:n], op=ALU.add),
                                 reads=b_pT[ppb] + [b_racc], writes=[b_raccb])
                        else:
                            P.op("dve", lambda e: e.tensor_tensor(out=racc[:, :, :, 0:n], in0=racc[:, :, :, 0:n], in1=pT[ppb][:, :, :, 0:n], op=ALU.add),
                                 reads=b_pT[ppb] + [b_racc], writes=[b_racc])
                    pend = (kp, pb) if kp < 17 else None
                pending_fin = make_fin(h, qi, q0, n)
        pending_fin[0]()
        pending_fin[1]()
        if debug:
            P.dma("sp", dbgO[:, :, :], oall[:, :, :], "dbg", reads=b_oall)
        P.barrier()

    if upto < 3:
        P.emit()
        ms.close()
        gs.close()
        return nc

    with ExitStack() as ls:
        gsb = sb("gsb", [128, 32, 512], BF16, ls)
        b_gsb = Buf("gsb")
        wo = sb("wo", [128, 8, D], BF16, ls)
        b_wo = Buf("wo")
        P.dma("sp", gsb[:, :, :], G_d.rearrange("j p n -> p j n"), "gsb", writes=[b_gsb])
        P.dma("pool", wo[:, :, :], w_out.rearrange("(c p) n -> p c n", p=128), "wo", writes=[b_wo])
        dpc = [sb("dpc%d" % i, [128, 4, 512], BF16, ls) for i in range(4)]
        b_dpc = [Buf("dpc%d" % i) for i in range(4)]
        osq = [sb("osq%d" % i, [128, 512], BF16, ls) for i in range(2)]
        b_osq = [Buf("osq%d" % i) for i in range(2)]
        ors = [sb("ors%d" % i, [128, 512], F32, ls) for i in range(2)]
        b_ors = [Buf("ors%d" % i) for i in range(2)]
        otb = [sb("otb%d" % i, [128, 8, 512], BF16, ls) for i in range(2)]
        b_otb = [Buf("otb%d" % i) for i in range(2)]
        ring = 0
        for qi, (q0, n) in enumerate(QB):
            s = qi % 2
            if n == 512:
                P.dma("sp", X[:, :, q0:q0 + 512], xT[:, :, q0:q0 + 512], "xr%d" % s, writes=[b_X[qi]])
            else:
                P.dma("sp", X[:, :, q0:q0 + 16], xT[:, :, 4080:4096], "xr%d" % s, writes=[b_X[qi]])
                P.dma("sp", X[:, :, q0 + 16:q0 + 32], xT[:, :, 2048:2064], "xr%d" % s, writes=[b_X[qi]])
            for h in range(NH):
                bk = h % 2
                P.op("act", lambda e: e.activation(out=osq[bk][:, 0:n], in_=oall[:, h, q0:q0 + n], func=AF.Square), reads=[b_oall[qi]], writes=[b_osq[bk]])
                P.op("pe", lambda e: e.matmul(ps[:, bk, 0:n], lhsT=ones[:, :], rhs=osq[bk][:, 0:n], start=True, stop=True),
                     reads=[b_ones, b_osq[bk]], writes=[PSB[bk]])
                P.op("act", lambda e: e.activation(out=ors[bk][:, 0:n], in_=ps[:, bk, 0:n], func=AF.Ln, bias=epsb[:, 0:1], scale=1.0 / 128),
                     reads=[PSB[bk], b_eps], writes=[b_ors[bk]])
                P.op("act", lambda e: e.activation(out=ors[bk][:, 0:n], in_=ors[bk][:, 0:n], func=AF.Exp, scale=-0.5), reads=[b_ors[bk]], writes=[b_ors[bk]])
                P.op("dve", lambda e: e.scalar_tensor_tensor(out=otb[s][:, h, 0:n], in0=oall[:, h, q0:q0 + n], scalar=lamt[:, 3:4], in1=ors[bk][:, 0:n],
                                                             op0=ALU.mult, op1=ALU.mult), reads=[b_oall[qi], b_lam, b_ors[bk]], writes=[b_otb[s]])
            step = 0
            for tab, src in ((0, dftC), (1, dftS)):
                for pc in range(8):
                    r = ring % 4
                    ring += 1
                    P.dma("sp", dpc[r][:, :, 0:n], src[pc * 4:(pc + 1) * 4, :, q0:q0 + n].rearrange("j p n -> p j n"), "dpc%d" % r, writes=[b_dpc[r]])
                    for j in range(4):
                        tc_ = pc * 4 + j
                        for m in range(2):
                            P.op("pe", lambda e, tab=tab, tc_=tc_, m=m, r=r, j=j, step=step: e.matmul(
                                ps[:, 2 + m, 0:n], lhsT=gsb[:, tc_, tab * 256 + m * 128: tab * 256 + (m + 1) * 128], rhs=dpc[r][:, j, 0:n],
                                start=(step == 0), stop=(step == 63)), reads=[b_gsb, b_dpc[r]], writes=[PSB[2 + m]])
                        step += 1
            for m in range(2):
                P.op("act", lambda e, m=m: e.activation(out=otb[s][:, 6 + m, 0:n], in_=ps[:, 2 + m, 0:n], func=AF.Copy), reads=[PSB[2 + m]], writes=[b_otb[s]])
            for g in range(8):
                bk = 4 + g % 4
                for c in range(8):
                    P.op("pe", lambda e, g=g, c=c, bk=bk: e.matmul(ps[:, bk, 0:n], lhsT=wo[:, c, g * 128:(g + 1) * 128], rhs=otb[s][:, c, 0:n],
                                                                  start=(c == 0), stop=(c == 7)), reads=[b_wo, b_otb[s]], writes=[PSB[bk]])
                P.op("dve", lambda e, g=g, bk=bk: e.scalar_tensor_tensor(out=X[:, g, q0:q0 + n], in0=ps[:, bk, 0:n], scalar=drv[:, 0, 2, g:g + 1], in1=X[:, g, q0:q0 + n],
                                                                         op0=ALU.mult, op1=ALU.add), reads=[PSB[bk], b_drv, b_drv2, b_X[qi]], writes=[b_X[qi]])
        if debug and upto == 3:
            P.dma("sp", dbgX[:, :, :], X[:, :, :], "dbg", reads=b_X)
        P.barrier()
    ms.close()

    b_Hs = [Buf("Hs%d" % i) for i in range(5)]
    PIECES = [4, 4, 4, 4, 3, 3]

    def make_norm(li, which, ls):
        Hs = sb("Hs", [128, 8, NQ], BF16, ls)
        tmp = sb("nrm_tmp", [128, 8, 512], F32, ls)
        b_tmp = Buf("nrm_tmp")
        sq = sb("nrm_sq", [128, 8, 512], BF16, ls)
        b_sq = Buf("nrm_sq")
        rstd = sb("nrm_rstd", [128, 512], F32, ls)
        b_rstd = Buf("nrm_rstd")
        a_i, b_i = (0, 1) if which == "mix" else (3, 4)

        def norm_a(qi):
            q0, n = QB[qi]
            rms_a(X[:, :, q0:q0 + n], [b_X[qi]], sq[:, :, 0:n], b_sq)

        def norm_b(qi, psb, part=0):
            q0, n = QB[qi]
            if part in (0, 1):
                rms_b1(X[:, :, q0:q0 + n], n, [b_X[qi]], tmp[:, :, 0:n], b_tmp, sq[:, :, 0:n], b_sq, rstd[:, 0:n], b_rstd, psb)
            if part in (0, 2):
                rms_b2(drv[:, li, a_i, :], drv[:, li, b_i, :], [b_drv, b_drv2], Hs[:, :, q0:q0 + n], b_Hs[qi], tmp[:, :, 0:n], b_tmp)
        norm_b.bufs = (tmp, b_tmp, sq, b_sq, rstd, b_rstd)
        return Hs, norm_a, norm_b

    def ffn(li, nblocks, final=False):
        with ExitStack() as ls:
            Hs, norm_a, norm_b = make_norm(li, "ffn", ls)
            if final:
                ob = [sb("ob0", [128, 8, 512], F32, ls)] * 2
                b_ob = [Buf("ob0")] * 2
                f_tmp, fb_tmp, f_sq, fb_sq, f_rstd, fb_rstd = norm_b.bufs
            norm_a(0)
            norm_b(0, 7)
            w1p = [sb("w1p%d" % i, [128, 8, 512], BF16, ls) for i in range(2)]
            w3p = [sb("w3p%d" % i, [128, 8, 512], BF16, ls) for i in range(2)]
            w2p = [sb("w2p%d" % i, [128, 4, D], BF16, ls) for i in range(2)]
            b_w1p = [Buf("w1p%d" % i) for i in range(2)]
            b_w3p = [Buf("w3p%d" % i) for i in range(2)]
            b_w2p = [Buf("w2p%d" % i) for i in range(2)]
            hid = [sb("hid%d" % i, [128, 4, 512], BF16, ls) for i in range(2)]
            b_hid = [Buf("hid%d" % i) for i in range(2)]
            sl = [sb("sl%d" % i, [128, 512], F32, ls) for i in range(2)]
            b_sl = [Buf("sl%d" % i) for i in range(2)]
            w1v = ffn_w1.rearrange("l (c p) n -> l p c n", p=128)
            w3v = ffn_w3.rearrange("l (c p) n -> l p c n", p=128)
            w2v = ffn_w2.rearrange("l (c p) n -> l p c n", p=128)
            c0 = 0
            it = 0
            for pi, m in enumerate(PIECES):
                s = pi % 2
                P.dma("pool", w1p[s][:, :, 0:128 * m], w1v[li, :, :, c0 * 128:(c0 + m) * 128], "w1p%d" % s, writes=[b_w1p[s]])
                P.dma("pool", w3p[s][:, :, 0:128 * m], w3v[li, :, :, c0 * 128:(c0 + m) * 128], "w3p%d" % s, writes=[b_w3p[s]])
                P.dma("pool", w2p[s][:, 0:m, :], w2v[li, :, c0:c0 + m, :], "w2p%d" % s, writes=[b_w2p[s]])
                for qi in range(nblocks):
                    q0, n = QB[qi]
                    hs = it % 2
                    it += 1
                    pipe_norm = (pi == 0 and qi + 1 < nblocks)
                    if pipe_norm:
                        norm_a(qi + 1)
                    for j in range(m):
                        if pipe_norm and j == 1:
                            norm_b(qi + 1, 7, part=1)
                        if pipe_norm and j == m - 1:
                            norm_b(qi + 1, 7, part=2)
                        bs = (it + j) % 2
                        for k in range(8):
                            P.op("pe", lambda e, j=j, k=k, bs=bs: e.matmul(ps[:, 2 * bs, 0:n], lhsT=w1p[s][:, k, j * 128:(j + 1) * 128], rhs=Hs[:, k, q0:q0 + n],
                                                                          start=(k == 0), stop=(k == 7)), reads=[b_w1p[s], b_Hs[qi]], writes=[PSB[2 * bs]])
                        for k in range(8):
                            P.op("pe", lambda e, j=j, k=k, bs=bs: e.matmul(ps[:, 2 * bs + 1, 0:n], lhsT=w3p[s][:, k, j * 128:(j + 1) * 128], rhs=Hs[:, k, q0:q0 + n],
                                                                          start=(k == 0), stop=(k == 7)), reads=[b_w3p[s], b_Hs[qi]], writes=[PSB[2 * bs + 1]])
                        P.op("act", lambda e, bs=bs: e.activation(out=sl[bs][:, 0:n], in_=ps[:, 2 * bs, 0:n], func=AF.Silu), reads=[PSB[2 * bs]], writes=[b_sl[bs]])
                        P.op("dve", lambda e, j=j, bs=bs, hs=hs: e.tensor_tensor(out=hid[hs][:, j, 0:n], in0=ps[:, 2 * bs + 1, 0:n], in1=sl[bs][:, 0:n], op=ALU.mult),
                             reads=[PSB[2 * bs + 1], b_sl[bs]], writes=[b_hid[hs]])
                    for g in range(8):
                        bk = 4 + g % 3
                        for j in range(m):
                            P.op("pe", lambda e, g=g, j=j, bk=bk, hs=hs: e.matmul(ps[:, bk, 0:n], lhsT=w2p[s][:, j, g * 128:(g + 1) * 128], rhs=hid[hs][:, j, 0:n],
                                                                                 start=(j == 0), stop=(j == m - 1)), reads=[b_w2p[s], b_hid[hs]], writes=[PSB[bk]])
                        P.op("dve", lambda e, g=g, bk=bk: e.scalar_tensor_tensor(out=X[:, g, q0:q0 + n], in0=ps[:, bk, 0:n], scalar=drv[:, li, 5, g:g + 1], in1=X[:, g, q0:q0 + n],
                                                                                 op0=ALU.mult, op1=ALU.add), reads=[PSB[bk], b_drv, b_drv2, b_X[qi]], writes=[b_X[qi]])
                    if final and pi == len(PIECES) - 1:
                        so = qi % 2
                        rms_mod(X[:, :, q0:q0 + 512], 512, [b_X[qi]], v("finalg"), None, [b_vec], ob[so][:, :, :], b_ob[so],
                                f_tmp[:, :, :], fb_tmp, f_sq[:, :, :], fb_sq, f_rstd[:, :], fb_rstd, 7)
                        P.dma("sp", outT[:, :, q0:q0 + 512], ob[so][:, :, :], "ob0", reads=[b_ob[so]])
                c0 += m
            P.barrier()

    if upto >= 4:
        ffn(0, 5)
        if debug and upto == 4:
            P.dma("sp", dbgX[:, :, :], X[:, :, :], "dbg", reads=b_X)
            P.barrier()

    if upto >= 5:
        with ExitStack() as ls:
            U = sb("U", [128, 8, NQ], BF16, ls)
            b_U = Buf("U")
            with ExitStack() as l2:
                Hs, norm_a, norm_b = make_norm(1, "mix", l2)
                norm_a(0)
                norm_b(0, 7)
                pw1 = sb("pw1", [128, 8, 2 * D], BF16, l2)
                b_pw1 = Buf("pw1")
                P.dma("pool", pw1[:, :, :], pw1_w.rearrange("(c p) n -> p c n", p=128), "pw1", writes=[b_pw1])
                sgt = [sb("sgt%d" % i, [128, 512], F32, l2) for i in range(2)]
                b_sgt = [Buf("sgt%d" % i) for i in range(2)]
                pb = VC["pw1b"][0]
                it = 0
                for qi, (q0, n) in enumerate(QB):
                    if qi + 1 < 5:
                        norm_a(qi + 1)
                    for j in range(8):
                        if qi + 1 < 5 and j == 1:
                            norm_b(qi + 1, 7, part=1)
                        if qi + 1 < 5 and j == 5:
                            norm_b(qi + 1, 7, part=2)
                        bs = it % 2
                        it += 1
                        for k in range(8):
                            P.op("pe", lambda e, j=j, k=k, bs=bs: e.matmul(ps[:, 2 * bs, 0:n], lhsT=pw1[:, k, j * 128:(j + 1) * 128], rhs=Hs[:, k, q0:q0 + n],
                                                                          start=(k == 0), stop=(k == 7)), reads=[b_pw1, b_Hs[qi]], writes=[PSB[2 * bs]])
                        for k in range(8):
                            P.op("pe", lambda e, j=j, k=k, bs=bs: e.matmul(ps[:, 2 * bs + 1, 0:n], lhsT=pw1[:, k, D + j * 128:D + (j + 1) * 128], rhs=Hs[:, k, q0:q0 + n],
                                                                          start=(k == 0), stop=(k == 7)), reads=[b_pw1, b_Hs[qi]], writes=[PSB[2 * bs + 1]])
                        P.op("act", lambda e, j=j, bs=bs: e.activation(out=sgt[bs][:, 0:n], in_=ps[:, 2 * bs + 1, 0:n], func=AF.Sigmoid, bias=vec[:, pb + 8 + j:pb + 9 + j]),
                             reads=[PSB[2 * bs + 1], b_vec], writes=[b_sgt[bs]])
                        if n == 512:
                            P.op("dve", lambda e, j=j, bs=bs: e.scalar_tensor_tensor(out=U[:, j, 16 + q0:16 + q0 + 512], in0=ps[:, 2 * bs, 0:512], scalar=vec[:, pb + j:pb + j + 1],
                                                                                    in1=sgt[bs][:, 0:512], op0=ALU.add, op1=ALU.mult),
                                 reads=[PSB[2 * bs], b_vec, b_sgt[bs]], writes=[b_U])
                        else:
                            for (a0, u0, mk) in ((0, 0, "ml"), (16, 2064, "mr")):
                                P.op("dve", lambda e, j=j, bs=bs, a0=a0, u0=u0: e.scalar_tensor_tensor(
                                    out=U[:, j, u0:u0 + 16], in0=ps[:, 2 * bs, a0:a0 + 16], scalar=vec[:, pb + j:pb + j + 1],
                                    in1=sgt[bs][:, a0:a0 + 16], op0=ALU.add, op1=ALU.mult), reads=[PSB[2 * bs], b_vec, b_sgt[bs]], writes=[b_U])
                                P.op("dve", lambda e, j=j, u0=u0, mk=mk: e.tensor_scalar(out=U[:, j, u0:u0 + 16], in0=U[:, j, u0:u0 + 16], scalar1=v(mk), scalar2=None, op0=ALU.mult),
                                     reads=[b_U, b_vec], writes=[b_U])
                if debug and upto == 5:
                    P.dma("sp", dbgU[:, :, :], U[:, :, :], "dbg", reads=[b_U])
                P.barrier()
            pw2 = sb("pw2", [128, 8, D], BF16, ls)
            b_pw2 = Buf("pw2")
            P.dma("pool", pw2[:, :, :], pw2_w.rearrange("(c p) n -> p c n", p=128), "pw2", writes=[b_pw2])
            dg = [sb("dg%d" % i, [128, 31, 128], BF16, ls) for i in range(2)]
            b_dg = [Buf("dg%d" % i) for i in range(2)]
            vc = [sb("vc%d" % i, [128, 8, 512], F32, ls) for i in range(2)]
            b_vc = [Buf("vc%d" % i) for i in range(2)]
            vcb = sb("vcb", [128, 8, 512], BF16, ls)
            b_vcb = Buf("vcb")
            vsq = sb("vsq", [128, 8, 512], BF16, ls)
            b_vsq = Buf("vsq")
            mu = sb("mu", [128, 512], F32, ls)
            b_mu = Buf("mu")
            var = sb("var", [128, 512], F32, ls)
            b_var = Buf("var")
            sv, b_sv = vcb, b_vcb
            dwo = VC["dww"][0]
            dbo = VC["dwb"][0]
            lg = VC["lng"][0]
            lb = VC["lnb"][0]
            dstate = {"it": 0, "built": {}}

            def diag_build(qi, j):
                if (qi, j) in dstate["built"]:
                    return dstate["built"][(qi, j)]
                ds = dstate["it"] % 2
                dstate["it"] += 1
                P.op("dve", lambda e: e.tensor_tensor(out=dg[ds][:, :, :], in0=ident[:, :].unsqueeze(1).to_broadcast([128, 31, 128]),
                                                      in1=vec[:, dwo + j * 31:dwo + (j + 1) * 31].unsqueeze(2).to_broadcast([128, 31, 128]), op=ALU.mult),
                     reads=[b_ident, b_vec], writes=[b_dg[ds]])
                dstate["built"][(qi, j)] = ds
                return ds

            def conv(qi, chunks=range(8)):
                q0 = qi * 512
                vs_ = qi % 2
                for j in chunks:
                    ds = diag_build(qi, j)
                    bk = j % 2
                    for tap in range(31):
                        P.op("pe", lambda e: e.matmul(ps[:, bk, :], lhsT=dg[ds][:, tap, :], rhs=U[:, j, q0 + tap + 1:q0 + tap + 1 + 512],
                                                      start=(tap == 0), stop=(tap == 30)), reads=[b_dg[ds], b_U], writes=[PSB[bk]])
                    P.op("act", lambda e: e.activation(out=vc[vs_][:, j, :], in_=ps[:, bk, :], func=AF.Identity, bias=vec[:, dbo + j:dbo + j + 1]),
                         reads=[PSB[bk], b_vec], writes=[b_vc[vs_]])

            def ln_a1(qi):
                vs_ = qi % 2
                P.op("act", lambda e: e.activation(out=vsq[:, :, :], in_=vc[vs_][:, :, :], func=AF.Square), reads=[b_vc[vs_]], writes=[b_vsq])
                P.op("dve", lambda e: e.tensor_copy(out=vcb[:, :, :], in_=vc[vs_][:, :, :]), reads=[b_vc[vs_]], writes=[b_vcb])

            def ln_a2(qi):
                for c in range(8):
                    P.op("pe", lambda e: e.matmul(ps[:, 2, :], lhsT=ones[:, :], rhs=vcb[:, c, :], start=(c == 0), stop=(c == 7)), reads=[b_ones, b_vcb], writes=[PSB[2]])
                for c in range(8):
                    P.op("pe", lambda e: e.matmul(ps[:, 3, :], lhsT=ones[:, :], rhs=vsq[:, c, :], start=(c == 0), stop=(c == 7)), reads=[b_ones, b_vsq], writes=[PSB[3]])

            def ln_b(qi):
                vs_ = qi % 2
                P.op("act", lambda e: e.activation(out=mu[:, :], in_=ps[:, 2, :], func=AF.Copy, scale=1.0 / D), reads=[PSB[2]], writes=[b_mu])
                P.op("dve", lambda e: e.tensor_tensor(out=var[:, :], in0=mu[:, :], in1=mu[:, :], op=ALU.mult), reads=[b_mu], writes=[b_var])
                P.op("dve", lambda e: e.scalar_tensor_tensor(out=var[:, :], in0=ps[:, 3, :], scalar=1.0 / D, in1=var[:, :], op0=ALU.mult, op1=ALU.subtract),
                     reads=[PSB[3], b_var], writes=[b_var])
                P.op("act", lambda e: e.activation(out=var[:, :], in_=var[:, :], func=AF.Ln, bias=epsb[:, 0:1]), reads=[b_var, b_eps], writes=[b_var])
                P.op("act", lambda e: e.activation(out=var[:, :], in_=var[:, :], func=AF.Exp, scale=-0.5), reads=[b_var], writes=[b_var])
                P.op("dve", lambda e: e.tensor_tensor(out=vc[vs_][:, :, :], in0=vc[vs_][:, :, :], in1=mu[:, :].unsqueeze(1).to_broadcast([128, 8, 512]), op=ALU.subtract),
                     reads=[b_vc[vs_], b_mu], writes=[b_vc[vs_]])
                P.op("dve", lambda e: e.tensor_tensor(out=vc[vs_][:, :, :], in0=vc[vs_][:, :, :], in1=var[:, :].unsqueeze(1).to_broadcast([128, 8, 512]), op=ALU.mult),
                     reads=[b_vc[vs_], b_var], writes=[b_vc[vs_]])
                for c in range(8):
                    P.op("act", lambda e: e.activation(out=sv[:, c, :], in_=vc[vs_][:, c, :], func=AF.Silu, bias=vec[:, lb + c:lb + c + 1], scale=vec[:, lg + c:lg + c + 1]),
                         reads=[b_vc[vs_], b_vec], writes=[b_sv])

            def pw2f(qi):
                q0 = qi * 512
                for g in range(8):
                    bk = 4 + g % 4
                    for c in range(8):
                        P.op("pe", lambda e: e.matmul(ps[:, bk, :], lhsT=pw2[:, c, g * 128:(g + 1) * 128], rhs=sv[:, c, :], start=(c == 0), stop=(c == 7)),
                             reads=[b_pw2, b_sv], writes=[PSB[bk]])
                    P.op("dve", lambda e: e.scalar_tensor_tensor(out=X[:, g, q0:q0 + 512], in0=ps[:, bk, :], scalar=drv[:, 1, 2, g:g + 1], in1=X[:, g, q0:q0 + 512],
                                                                 op0=ALU.mult, op1=ALU.add), reads=[PSB[bk], b_drv, b_drv2, b_X[qi]], writes=[b_X[qi]])
                    P.op("dve", lambda e: e.tensor_scalar(out=X[:, g, q0:q0 + 512], in0=X[:, g, q0:q0 + 512], scalar1=drv[:, 1, 6, g:g + 1], scalar2=None, op0=ALU.add),
                         reads=[b_drv, b_drv2, b_X[qi]], writes=[b_X[qi]])

            conv(0)
            for qi in range(4):
                ln_a1(qi)
                if qi + 1 < 4:
                    conv(qi + 1, range(0, 3))
                ln_a2(qi)
                if qi + 1 < 4:
                    diag_build(qi + 1, 3)
                    diag_build(qi + 1, 4)
                ln_b(qi)
                if qi + 1 < 4:
                    conv(qi + 1, range(3, 8))
                pw2f(qi)
            if debug and upto == 5:
                P.dma("sp", dbgX[:, :, :], X[:, :, :], "dbg", reads=b_X)
            P.barrier()

    if upto >= 6:
        ffn(1, 4, final=True)

    P.emit()
    gs.close()
    return nc


def _const_tables(half):
    pos_tok = np.concatenate([(np.arange(SEQ) + OWN * half) % SEQ])
    qpos = np.concatenate([np.arange(OWN), np.arange(4080, 4096), np.arange(2048, 2064)])
    qtok = pos_tok[qpos]
    inv = (10000.0 ** (-np.arange(0, 32, 2, dtype=np.float32) / np.float32(32))).astype(np.float32)

    def tabs(tok):
        row = (tok // 64).astype(np.float32)
        col = (tok % 64).astype(np.float32)
        ang = np.concatenate([row[:, None] * inv, col[:, None] * inv], axis=-1).astype(np.float32)
        return np.cos(ang).astype(np.float32), np.sin(ang).astype(np.float32)

    pp = np.arange(128)
    dd = pp % 64
    ii = dd % 32
    sign = np.where(dd < 32, -1.0, 1.0).astype(np.float32)
    ck, sk = tabs(pos_tok)
    ropeK = np.zeros((128, 2, NT), np.float32)
    ropeK[:, 0, :SEQ] = ck[:, ii].T
    ropeK[:, 1, :SEQ] = sk[:, ii].T * sign[:, None]
    ropeK[:, 0, SEQ:] = 1.0
    cq, sq = tabs(qtok)
    ropeQ = np.zeros((128, 2, NQ), np.float32)
    ropeQ[:, 0, :] = cq[:, ii].T * 0.125
    ropeQ[:, 1, :] = sq[:, ii].T * sign[:, None] * 0.125
    prod = (pos_tok[:, None].astype(np.int64) * qtok[None, :].astype(np.int64)) % SEQ
    angd = 2.0 * np.pi * prod / SEQ
    dftC = (np.cos(angd) / 64.0).astype(np.float32).reshape(32, 128, NQ).astype(ml_dtypes.bfloat16)
    dftS = (-np.sin(angd) / 64.0).astype(np.float32).reshape(32, 128, NQ).astype(ml_dtypes.bfloat16)
    return ropeK, ropeQ, dftC, dftS


def _ccss():
    l = np.arange(64)
    a = 2.0 * np.pi * ((l[:, None] * l[None, :]) % 64) / 64.0
    cc = np.cos(a) / 8.0
    ss = np.sin(a) / 8.0
    out = np.zeros((256, 512), np.float32)
    for g in range(4):
        out[g * 64:(g + 1) * 64, g * 64:(g + 1) * 64] = cc
        out[g * 64:(g + 1) * 64, 256 + g * 64:256 + (g + 1) * 64] = ss
    return out.astype(ml_dtypes.bfloat16)


def _chunkT(vec1d):
    return np.ascontiguousarray(vec1d.reshape(-1, 128).T)


def prep_inputs(inp):
    f32 = np.float32
    x = np.asarray(inp["x"], f32)
    ctx = np.asarray(inp["ctx"], f32)
    w_in = np.asarray(inp["ev_w_in"], f32)[0]
    wq = w_in[:, 0:768]
    wk = w_in[:, 768:1536]
    wv = w_in[:, 1536:2304]
    wf = w_in[:, 2304:2560]
    swap = np.concatenate([np.arange(64 * i + 32, 64 * i + 64).tolist() + np.arange(64 * i, 64 * i + 32).tolist() for i in range(12)]).astype(np.int64)
    w_ext = np.ascontiguousarray(np.concatenate([wq, wk, wv], axis=1))
    permf = np.zeros((128, 128), f32)
    permf[swap[:128], np.arange(128)] = 1.0
    w_fT = np.ascontiguousarray(wf.T)
    fw = np.asarray(inp["ev_fourier_w"], f32)[0]
    fw_bd = np.zeros((256, 256), f32)
    for g in range(4):
        fw_bd[g * 64:(g + 1) * 64, g * 64:(g + 1) * 64] = fw[g]
    shared = {
        "ada_w": np.ascontiguousarray(np.asarray(inp["ada_w"], f32)),
        "w_ext": w_ext, "permf": permf, "w_fT": w_fT, "fw_bd": fw_bd, "ccss": _ccss(),
        "identb": np.eye(128, dtype=f32).astype(ml_dtypes.bfloat16),
        "w_out": np.ascontiguousarray(np.asarray(inp["ev_w_out"], f32)[0]),
        "ffn_w1": np.ascontiguousarray(np.asarray(inp["ffn_w1"], f32)),
        "ffn_w3": np.ascontiguousarray(np.asarray(inp["ffn_w3"], f32)),
        "ffn_w2": np.ascontiguousarray(np.asarray(inp["ffn_w2"], f32)),
        "pw1_w": np.ascontiguousarray(np.asarray(inp["od_pw1_w"], f32)[0]),
        "pw2_w": np.ascontiguousarray(np.asarray(inp["od_pw2_w"], f32)[0]),
    }
    consts = [_const_tables(h) for h in range(2)]
    vbase = np.zeros((128, NV), f32)

    def put(name, arr):
        o, w = VC[name]
        assert arr.shape == (128, w), (name, arr.shape)
        vbase[:, o:o + w] = arr

    put("ada_b0", _chunkT(np.asarray(inp["ada_b"], f32)[0]))
    put("ada_b1", _chunkT(np.asarray(inp["ada_b"], f32)[1]))
    put("mixg0", _chunkT(np.asarray(inp["mix_norm_g"], f32)[0]))
    put("mixg1", _chunkT(np.asarray(inp["mix_norm_g"], f32)[1]))
    put("ffng0", _chunkT(np.asarray(inp["ffn_norm_g"], f32)[0]))
    put("ffng1", _chunkT(np.asarray(inp["ffn_norm_g"], f32)[1]))
    put("finalg", _chunkT(np.asarray(inp["final_g"], f32)))
    put("pw1b", _chunkT(np.asarray(inp["od_pw1_b"], f32)[0]))
    put("dwb", _chunkT(np.asarray(inp["od_dw_b"], f32)[0]))
    put("lng", _chunkT(np.asarray(inp["od_ln_g"], f32)[0]))
    put("lnb", _chunkT(np.asarray(inp["od_ln_b"], f32)[0]))
    put("pw2b", _chunkT(np.asarray(inp["od_pw2_b"], f32)[0]))
    put("sublng", np.asarray(inp["ev_subln_g"], f32)[0].reshape(128, 1))
    dw = np.asarray(inp["od_dw_w"], f32)[0]
    put("dww", np.ascontiguousarray(dw.T.reshape(8, 128, 31).transpose(1, 0, 2).reshape(128, 248)))
    lamrow = np.concatenate([np.asarray(inp[k], f32)[0] for k in ("ev_lambda_q1", "ev_lambda_k1", "ev_lambda_q2", "ev_lambda_k2")])
    put("lam", np.broadcast_to(lamrow[None, :], (128, 256)))
    c_ctx = np.asarray(inp["c_ctx"], f32)
    in_maps = []
    for core in range(8):
        b, half = core // 2, core % 2
        seq = np.roll(x[b], -OWN * half, axis=0)
        full = np.concatenate([seq, ctx[b]], axis=0)
        xT = np.ascontiguousarray(full.T.reshape(8, 128, NT).transpose(1, 0, 2))
        cvec = np.zeros((128, 16), f32)
        cvec[:, 0::2] = _chunkT(np.asarray(inp["c"], f32)[b])
        cvec[:, 1::2] = _chunkT(c_ctx)
        vv = vbase.copy()
        vv[:, VC["ml"][0]] = 1.0 if half == 1 else 0.0
        vv[:, VC["mr"][0]] = 1.0 if half == 0 else 0.0
        ropeK, ropeQ, dftC, dftS = consts[half]
        m = dict(shared)
        m.update({"xT": xT, "cvec": cvec, "vecs": vv, "ropeK": ropeK, "ropeQ": ropeQ, "dftC": dftC, "dftS": dftS})
        in_maps.append(m)
    return in_maps


_NC_CACHE = {}


def kernel(**inputs):
    in_maps = prep_inputs(inputs)
    if "nc" not in _NC_CACHE:
        _NC_CACHE["nc"] = build_program()
    nc = _NC_CACHE["nc"]
    res = run_bass_kernel_spmd(nc, in_maps, core_ids=list(range(8)))
    out = np.zeros((4, SEQ, D), np.float32)
    for core in range(8):
        b, half = core // 2, core % 2
        oT = np.asarray(res.results[core]["outT"], np.float32)
        out[b, half * OWN:(half + 1) * OWN, :] = oT.transpose(2, 1, 0).reshape(OWN, D)
    return out
```

```python
import numpy as np
import ml_dtypes
from contextlib import ExitStack
import concourse.bass as bass
import concourse.mybir as mybir
from concourse.bass_utils import run_bass_kernel_spmd

F32 = mybir.dt.float32
BF16 = mybir.dt.bfloat16
AF = mybir.ActivationFunctionType
ALU = mybir.AluOpType

D = 1024
SEQ = 4096
CTX = 256
NT = SEQ + CTX
OWN = 2048
HALO = 16
NQ = OWN + 2 * HALO
DFF = 2816
NH = 6
EPS = 1e-6
LAMBDA_INIT0 = 0.8 - 0.6 * 1.0

VC = {}
_off = 0
for _n, _w in [("ada_b0", 48), ("ada_b1", 48), ("mixg0", 8), ("mixg1", 8), ("ffng0", 8), ("ffng1", 8),
               ("finalg", 8), ("pw1b", 16), ("dwb", 8), ("lng", 8), ("lnb", 8), ("pw2b", 8),
               ("sublng", 1), ("ml", 1), ("mr", 1), ("dww", 248), ("lam", 256)]:
    VC[_n] = (_off, _w)
    _off += _w
NV = _off


class Buf:
    __slots__ = ("name", "w", "r")

    def __init__(self, name):
        self.name = name
        self.w = None
        self.r = []


class Op:
    __slots__ = ("eng", "fn", "deps", "idx", "needed", "val", "dma_sem", "dma_val", "waits")


class _Rec:
    def __init__(self):
        self.call = None

    def __getattr__(self, name):
        def f(*a, **k):
            self.call = (name, a, k)
            return None
        return f


class Prog:
    ENG = ("pe", "act", "dve", "pool", "sp")
    BLK = {"pe": "tensor", "act": "scalar", "dve": "vector", "pool": "gpsimd", "sp": "sync"}

    def __init__(self, nc):
        self.nc = nc
        self.q = {e: [] for e in self.ENG}
        self.dma_cnt = {}
        self.dma_last = {}

    def _mk(self, eng, fn, reads, writes):
        o = Op()
        o.eng = eng
        rec = _Rec()
        fn(rec)
        assert rec.call is not None
        o.fn = rec.call
        o.dma_sem = None
        o.dma_val = 0
        o.needed = False
        o.val = 0
        deps = []
        for b in reads:
            if b.w is not None:
                deps.append(b.w)
        for b in writes:
            if b.w is not None:
                deps.append(b.w)
            for r in b.r:
                if r.dma_sem is None and r.eng == eng:
                    continue
                deps.append(r)
        o.deps = deps
        o.idx = len(self.q[eng])
        self.q[eng].append(o)
        for b in reads:
            b.r.append(o)
        for b in writes:
            b.w = o
            b.r = []
        return o

    def op(self, eng, fn, reads=(), writes=()):
        return self._mk(eng, fn, reads, writes)

    def dma(self, eng, out_ap, in_ap, sem, reads=(), writes=()):
        o = self._mk(eng, lambda e: e.dma_start(out=out_ap, in_=in_ap), reads, writes)
        self.dma_cnt[sem] = self.dma_cnt.get(sem, 0) + 1
        o.dma_sem = sem
        o.dma_val = 16 * self.dma_cnt[sem]
        self.dma_last[sem] = o
        return o

    def barrier(self):
        o = Op()
        o.eng = "sp"
        o.fn = None
        o.dma_sem = None
        o.dma_val = 0
        o.needed = False
        o.val = 0
        o.deps = [self.q[e][-1] for e in self.ENG if e != "sp" and self.q[e]] + list(self.dma_last.values())
        o.idx = len(self.q["sp"])
        self.q["sp"].append(o)
        for e in self.ENG:
            if e == "sp":
                continue
            w = Op()
            w.eng = e
            w.fn = None
            w.dma_sem = None
            w.dma_val = 0
            w.needed = False
            w.val = 0
            w.deps = [o]
            w.idx = len(self.q[e])
            self.q[e].append(w)

    def emit(self):
        nc = self.nc
        for eng in self.ENG:
            seen = {}
            for o in self.q[eng]:
                waits = []
                for d in o.deps:
                    if d.dma_sem is not None:
                        key = ("d", d.dma_sem)
                        v = d.dma_val
                    else:
                        if d.eng == eng and eng == "pe":
                            continue
                        key = ("e", d.eng)
                        v = d.idx
                    if seen.get(key, -1) >= v:
                        continue
                    seen[key] = v
                    waits.append(d)
                    if d.dma_sem is None:
                        d.needed = True
                o.waits = waits
        for eng in self.ENG:
            c = 0
            for o in self.q[eng]:
                if o.needed:
                    c += 1
                    o.val = c
        with ExitStack() as st:
            esem = {e: st.enter_context(nc.semaphore("e_" + e)) for e in self.ENG}
            dsem = {s: st.enter_context(nc.semaphore("d_" + s)) for s in self.dma_cnt}
            block = st.enter_context(nc.Block())
            for eng in self.ENG:
                def body(e, eng=eng):
                    for o in self.q[eng]:
                        for d in o.waits:
                            if d.dma_sem is not None:
                                e.wait_ge(dsem[d.dma_sem], d.dma_val)
                            else:
                                e.wait_ge(esem[d.eng], d.val)
                        ins = None
                        if o.fn is not None:
                            m_, a_, k_ = o.fn
                            ins = getattr(e, m_)(*a_, **k_)
                        if o.dma_sem is not None:
                            ins.then_inc(dsem[o.dma_sem], 16)
                        if o.needed:
                            if ins is not None and o.dma_sem is None:
                                ins.then_inc(esem[eng], 1)
                            else:
                                e.sem_inc(esem[eng], 1)
                getattr(block, self.BLK[eng])(body)


def build_program(upto=99, debug=False):
    nc = bass.Bass("TRN2", target_bir_lowering=False)
    P = Prog(nc)

    def din(name, shape, dt=F32):
        return nc.dram_tensor(name, list(shape), dt, kind="ExternalInput").ap()

    xT = din("xT", [128, 8, NT])
    cvec = din("cvec", [128, 16])
    vecs = din("vecs", [128, NV])
    ada_w = din("ada_w", [2, D, 6 * D])
    w_ext = din("w_ext", [D, 2304])
    permf_d = din("permf", [128, 128])
    w_fT = din("w_fT", [256, D])
    fw_bd = din("fw_bd", [256, 256])
    ccss = din("ccss", [256, 512], BF16)
    identb = din("identb", [128, 128], BF16)
    ropeK = din("ropeK", [128, 2, NT])
    ropeQ = din("ropeQ", [128, 2, NQ])
    dftC = din("dftC", [32, 128, NQ], BF16)
    dftS = din("dftS", [32, 128, NQ], BF16)
    w_out = din("w_out", [D, D])
    ffn_w1 = din("ffn_w1", [2, D, DFF])
    ffn_w3 = din("ffn_w3", [2, D, DFF])
    ffn_w2 = din("ffn_w2", [2, DFF, D])
    pw1_w = din("pw1_w", [D, 2 * D])
    pw2_w = din("pw2_w", [D, D])
    outT = nc.dram_tensor("outT", [128, 8, OWN], F32, kind="ExternalOutput").ap()

    skind = "ExternalOutput" if debug else "Internal"
    KT_d = nc.dram_tensor("KT_d", [NH, 128, NT], BF16, kind=skind).ap()
    V_d = nc.dram_tensor("V_d", [34, 128, 768], BF16, kind=skind).ap()
    G_d = nc.dram_tensor("G_d", [32, 128, 512], BF16, kind=skind).ap()
    QT_d = nc.dram_tensor("QT_d", [NH, 128, NQ], BF16, kind=skind).ap()
    if debug:
        dbgX = nc.dram_tensor("dbgX", [128, 8, NQ], F32, kind="ExternalOutput").ap()
        dbgO = nc.dram_tensor("dbgO", [128, NH, NQ], F32, kind="ExternalOutput").ap()
        dbgM = nc.dram_tensor("dbgM", [128, 200], F32, kind="ExternalOutput").ap()
        dbgU = nc.dram_tensor("dbgU", [128, 8, NQ], BF16, kind="ExternalOutput").ap()

    gs = ExitStack()

    uid = [0]

    def sb(name, shape, dt, stack=gs):
        uid[0] += 1
        return stack.enter_context(nc.sbuf_tensor("%s_%d" % (name, uid[0]), list(shape), dt))

    ps = gs.enter_context(nc.psum_tensor("ps", [128, 8, 512], F32))
    PSB = [Buf("ps%d" % i) for i in range(8)]

    vec = sb("vec", [128, NV], F32)
    b_vec = Buf("vec")
    ones = sb("ones", [128, 128], BF16)
    b_ones = Buf("ones")
    ident = sb("ident", [128, 128], BF16)
    b_ident = Buf("ident")
    sel = sb("sel", [64, 2, 128], F32)
    b_sel = Buf("sel")
    epsb = sb("epsb", [128, 1], F32)
    b_eps = Buf("epsb")
    mods = sb("mods", [128, 2, 2, 48], F32)
    b_mods = Buf("mods")
    drv = sb("drv", [128, 2, 7, 8], F32)
    b_drv = Buf("drv")
    b_drv2 = Buf("drv2")
    drvc = sb("drvc", [128, 2, 8], F32)
    b_drvc = Buf("drvc")
    lamt = sb("lamt", [128, 4], F32)
    b_lam = Buf("lamt")
    b_X = [Buf("X%d" % i) for i in range(5)]

    def v(name, a=0, b=None):
        o, w = VC[name]
        if b is None:
            b = w
        return vec[:, o + a:o + b]

    QB = [(i * 512, 512) for i in range(4)] + [(2048, 32)]

    P.dma("sp", vec[:, :], vecs[:, :], "vec", writes=[b_vec])
    P.dma("sp", ident[:, :], identb[:, :], "ident", writes=[b_ident])
    P.op("dve", lambda e: e.memset(ones[:, :], 1.0), writes=[b_ones])
    P.op("dve", lambda e: e.memset(epsb[:, :], EPS), writes=[b_eps])
    P.op("dve", lambda e: e.memset(sel[:, :, :], 0.0), writes=[b_sel])
    P.op("dve", lambda e: e.memset(sel[0:32, 0, :], 1.0 / 32.0), writes=[b_sel])
    P.op("dve", lambda e: e.memset(sel[32:64, 1, :], 1.0 / 32.0), writes=[b_sel])

    ws = ExitStack()
    scb = sb("scb", [128, 8, 2], BF16, ws)
    b_scb = Buf("scb")
    wa = [sb("wa%d" % i, [128, 8, 512], BF16, ws) for i in range(3)]
    b_wa = [Buf("wa%d" % i) for i in range(3)]
    ada_v = ada_w.rearrange("l (c p) n -> l p c n", p=128)
    ada_state = {"next": 0, "dma": 0}
    b_adaps = [Buf("adaps%d" % i) for i in range(3)]

    def ada_finish(li):
        mg = "mixg%d" % li
        fg = "ffng%d" % li
        P.op("dve", lambda e: e.scalar_tensor_tensor(
            out=drv[:, li, 3, :], in0=mods[:, li, 0, 32:40], scalar=1.0, in1=v(fg), op0=ALU.add, op1=ALU.mult),
            reads=[b_mods, b_vec], writes=[b_drv2])
        P.op("dve", lambda e: e.tensor_copy(out=drv[:, li, 2, :], in_=mods[:, li, 0, 16:24]), reads=[b_mods], writes=[b_drv2])
        P.op("dve", lambda e: e.tensor_copy(out=drv[:, li, 4, :], in_=mods[:, li, 0, 24:32]), reads=[b_mods], writes=[b_drv2])
        P.op("dve", lambda e: e.tensor_copy(out=drv[:, li, 5, :], in_=mods[:, li, 0, 40:48]), reads=[b_mods], writes=[b_drv2])
        P.op("dve", lambda e: e.tensor_tensor(out=drv[:, li, 6, :], in0=mods[:, li, 0, 16:24], in1=v("pw2b"), op=ALU.mult),
             reads=[b_mods, b_vec], writes=[b_drv2])

    def ada_first(li):
        mg = "mixg%d" % li
        P.op("dve", lambda e: e.scalar_tensor_tensor(
            out=drv[:, li, 0, :], in0=mods[:, li, 0, 8:16], scalar=1.0, in1=v(mg), op0=ALU.add, op1=ALU.mult),
            reads=[b_mods, b_vec], writes=[b_drv if li == 0 else b_drv2])
        P.op("dve", lambda e: e.tensor_copy(out=drv[:, li, 1, :], in_=mods[:, li, 0, 0:8]), reads=[b_mods], writes=[b_drv if li == 0 else b_drv2])
        if li == 0:
            P.op("dve", lambda e: e.scalar_tensor_tensor(
                out=drvc[:, 0, :], in0=mods[:, 0, 1, 8:16], scalar=1.0, in1=v("mixg0"), op0=ALU.add, op1=ALU.mult),
                reads=[b_mods, b_vec], writes=[b_drvc])
            P.op("dve", lambda e: e.tensor_copy(out=drvc[:, 1, :], in_=mods[:, 0, 1, 0:8]), reads=[b_mods], writes=[b_drvc])

    def ada_dma(k):
        for _ in range(k):
            it = ada_state["dma"]
            if it >= 24:
                return
            ada_state["dma"] = it + 1
            li, pc = it // 12, it % 12
            s_ = it % 3
            P.dma("pool", wa[s_][:, :, :], ada_v[li, :, :, pc * 512:(pc + 1) * 512], "wa%d" % s_, writes=[b_wa[s_]])

    def ada_jobs(k, psb=0):
        for _ in range(k):
            it = ada_state["next"]
            if it >= 24:
                return
            if ada_state["dma"] <= it:
                ada_dma(1)
            ada_state["next"] = it + 1
            li, pc = it // 12, it % 12
            s_ = it % 3
            for g in range(4):
                col = g * 2
                for k_ in range(8):
                    P.op("pe", lambda e: e.matmul(ps[:, psb, col:col + 2], lhsT=wa[s_][:, k_, g * 128:(g + 1) * 128], rhs=scb[:, k_, :],
                                                  start=(k_ == 0), stop=(k_ == 7)), reads=[b_wa[s_], b_scb], writes=[PSB[psb]])
            psv = ps[:, psb, 0:8].rearrange("p (g t) -> p g t", t=2)
            ao = VC["ada_b%d" % li][0]
            for t in range(2):
                P.op("dve", lambda e: e.tensor_tensor(out=mods[:, li, t, pc * 4:(pc + 1) * 4], in0=psv[:, :, t], in1=vec[:, ao + pc * 4:ao + (pc + 1) * 4], op=ALU.add),
                     reads=[PSB[psb], b_vec], writes=[b_mods])
            if pc == 3:
                ada_first(li)
            if pc == 11:
                ada_finish(li)

    with ExitStack() as ls:
        cv = sb("cv", [128, 16], F32, ls)
        b_cv = Buf("cv")
        P.dma("sp", cv[:, :], cvec[:, :], "cv", writes=[b_cv])
        sg = sb("sg", [128, 16], F32, ls)
        b_sg = Buf("sg")
        P.op("act", lambda e: e.activation(out=sg[:, :], in_=cv[:, :], func=AF.Sigmoid), reads=[b_cv], writes=[b_sg])
        P.op("dve", lambda e: e.tensor_tensor(out=scb[:, :, :].rearrange("p c k -> p (c k)"), in0=cv[:, :], in1=sg[:, :], op=ALU.mult),
             reads=[b_cv, b_sg], writes=[b_scb])
        lt = sb("lt", [128, 128], F32, ls)
        b_lt = Buf("lt")
        lo = VC["lam"][0]
        P.op("dve", lambda e: e.tensor_tensor(out=lt[:, 0:64], in0=vec[:, lo:lo + 64], in1=vec[:, lo + 64:lo + 128], op=ALU.mult),
             reads=[b_vec], writes=[b_lt])
        P.op("dve", lambda e: e.tensor_tensor(out=lt[:, 64:128], in0=vec[:, lo + 128:lo + 192], in1=vec[:, lo + 192:lo + 256], op=ALU.mult),
             reads=[b_vec], writes=[b_lt])
        ls2 = sb("ls2", [128, 2], F32, ls)
        b_ls2 = Buf("ls2")
        P.op("dve", lambda e: e.tensor_reduce(out=ls2[:, :], in_=lt[:, :].rearrange("p (a b) -> p a b", a=2),
                                              axis=mybir.AxisListType.X, op=ALU.add), reads=[b_lt], writes=[b_ls2])
        P.op("act", lambda e: e.activation(out=lamt[:, 0:2], in_=ls2[:, :], func=AF.Exp), reads=[b_ls2], writes=[b_lam])
        P.op("dve", lambda e: e.scalar_tensor_tensor(out=lamt[:, 2:3], in0=lamt[:, 1:2], scalar=-LAMBDA_INIT0, in1=lamt[:, 0:1],
                                                     op0=ALU.add, op1=ALU.subtract), reads=[b_lam], writes=[b_lam])
        P.op("dve", lambda e: e.tensor_scalar(out=lamt[:, 3:4], in0=v("sublng"), scalar1=1.0 - LAMBDA_INIT0, scalar2=None, op0=ALU.mult),
             reads=[b_vec], writes=[b_lam])
        ada_dma(3)
        ada_jobs(1)
        ada_dma(1)
        ada_jobs(3)
        if debug:
            ada_jobs(24)
            P.dma("sp", dbgM[:, 0:96], mods[:, 0, :, :].rearrange("p a b -> p (a b)"), "dbg", reads=[b_mods])
            P.dma("sp", dbgM[:, 96:192], mods[:, 1, :, :].rearrange("p a b -> p (a b)"), "dbg", reads=[b_mods])
            P.dma("sp", dbgM[:, 192:196], lamt[:, :], "dbg", reads=[b_lam])
        P.barrier()

    def rms_mod(src3, n, src_bufs, A, B, AB_bufs, hb3, hb_buf, tmp3, tmp_buf, sq3, sq_buf, rstd, rstd_buf, psb, out_f32=False):
        rms_a(src3, src_bufs, sq3, sq_buf)
        rms_b(src3, n, src_bufs, A, B, AB_bufs, hb3, hb_buf, tmp3, tmp_buf, sq3, sq_buf, rstd, rstd_buf, psb)

    def rms_a(src3, src_bufs, sq3, sq_buf):
        P.op("act", lambda e: e.activation(out=sq3, in_=src3, func=AF.Square), reads=src_bufs, writes=[sq_buf])

    def rms_b(src3, n, src_bufs, A, B, AB_bufs, hb3, hb_buf, tmp3, tmp_buf, sq3, sq_buf, rstd, rstd_buf, psb):
        rms_b1(src3, n, src_bufs, tmp3, tmp_buf, sq3, sq_buf, rstd, rstd_buf, psb)
        rms_b2(A, B, AB_bufs, hb3, hb_buf, tmp3, tmp_buf)

    def rms_b1(src3, n, src_bufs, tmp3, tmp_buf, sq3, sq_buf, rstd, rstd_buf, psb):
        for c in range(8):
            P.op("pe", lambda e, c=c: e.matmul(ps[:, psb, 0:n], lhsT=ones[:, :], rhs=sq3[:, c, :], start=(c == 0), stop=(c == 7)),
                 reads=[b_ones, sq_buf], writes=[PSB[psb]])
        P.op("act", lambda e: e.activation(out=rstd, in_=ps[:, psb, 0:n], func=AF.Ln, bias=epsb[:, 0:1], scale=1.0 / D),
             reads=[PSB[psb], b_eps], writes=[rstd_buf])
        P.op("act", lambda e: e.activation(out=rstd, in_=rstd, func=AF.Exp, scale=-0.5), reads=[rstd_buf], writes=[rstd_buf])
        P.op("dve", lambda e: e.tensor_tensor(out=tmp3, in0=src3, in1=rstd.unsqueeze(1).to_broadcast([128, 8, n]), op=ALU.mult),
             reads=list(src_bufs) + [rstd_buf], writes=[tmp_buf])

    def rms_b2(A, B, AB_bufs, hb3, hb_buf, tmp3, tmp_buf):
        for c in range(8):
            if B is None:
                P.op("dve", lambda e, c=c: e.tensor_scalar(out=hb3[:, c, :], in0=tmp3[:, c, :], scalar1=A[:, c:c + 1], scalar2=None, op0=ALU.mult),
                     reads=[tmp_buf] + AB_bufs, writes=[hb_buf])
            elif c % 2 == 0:
                P.op("act", lambda e, c=c: e.activation(out=hb3[:, c, :], in_=tmp3[:, c, :], func=AF.Identity,
                                                        bias=B[:, c:c + 1], scale=A[:, c:c + 1]),
                     reads=[tmp_buf] + AB_bufs, writes=[hb_buf])
            else:
                P.op("dve", lambda e, c=c: e.tensor_scalar(out=hb3[:, c, :], in0=tmp3[:, c, :], scalar1=A[:, c:c + 1], scalar2=B[:, c:c + 1],
                                                           op0=ALU.mult, op1=ALU.add),
                     reads=[tmp_buf] + AB_bufs, writes=[hb_buf])

    if upto < 1:
        P.emit()
        ws.close()
        gs.close()
        return nc

    with ExitStack() as ls:
        wq = sb("wq", [128, 8, 768], BF16, ls)
        wk = sb("wk", [128, 8, 768], BF16, ls)
        permf = sb("permf", [128, 128], F32, ls)
        b_permf = Buf("permf")
        P.dma("sp", permf[:, :], permf_d[:, :], "permf", writes=[b_permf])
        kf = [sb("kf%d" % i, [128, 512], F32, ls) for i in range(2)]
        b_kf = [Buf("kf%d" % i) for i in range(2)]
        wvg = sb("wvg", [128, 8, 1280], BF16, ls)
        b_wq, b_wk, b_wvg = Buf("wq"), Buf("wk"), Buf("wvg")
        wv_ = w_ext.rearrange("(c p) n -> p c n", p=128)
        if True:
            l2 = ls
            fwb = sb("fwb", [128, 2, 256], BF16, l2)
            ccb = sb("ccb", [128, 2, 512], BF16, l2)
            wft = sb("wft", [128, 2, D], BF16, l2)
            m12 = sb("m12", [128, 2, 512], BF16, l2)
            b_fwb, b_ccb, b_wft, b_m12 = Buf("fwb"), Buf("ccb"), Buf("wft"), Buf("m12")
            P.dma("pool", fwb[:, :, :], fw_bd.rearrange("(c p) n -> p c n", p=128), "fwb", writes=[b_fwb])
            P.dma("sp", ccb[:, :, :], ccss.rearrange("(c p) n -> p c n", p=128), "ccb", writes=[b_ccb])
            P.dma("pool", wft[:, :, :], w_fT.rearrange("(c p) n -> p c n", p=128), "wft", writes=[b_wft])
            P.dma("pool", wk[:, :, :], wv_[:, :, 768:1536], "wk", writes=[b_wk])
            P.dma("pool", wvg[:, :, 0:768], wv_[:, :, 1536:2304], "wvg", writes=[b_wvg])
            P.dma("pool", wq[:, :, :], wv_[:, :, 0:768], "wq", writes=[b_wq])
            if not debug:
                ada_dma(3)
            for cc in range(2):
                for half in range(2):
                    for lc in range(2):
                        P.op("pe", lambda e, cc=cc, half=half, lc=lc: e.matmul(
                            ps[:, 1 + cc, half * 256:(half + 1) * 256],
                            lhsT=ccb[:, lc, half * 256 + cc * 128: half * 256 + (cc + 1) * 128], rhs=fwb[:, lc, :],
                            start=(lc == 0), stop=(lc == 1)), reads=[b_ccb, b_fwb], writes=[PSB[1 + cc]])
                P.op("act", lambda e, cc=cc: e.activation(out=m12[:, cc, :], in_=ps[:, 1 + cc, :], func=AF.Copy),
                     reads=[PSB[1 + cc]], writes=[b_m12])
            for dc in range(8):
                bk = 3 + dc % 2
                for cc in range(2):
                    P.op("pe", lambda e, dc=dc, cc=cc, bk=bk: e.matmul(
                        ps[:, bk, :], lhsT=wft[:, cc, dc * 128:(dc + 1) * 128], rhs=m12[:, cc, :],
                        start=(cc == 0), stop=(cc == 1)), reads=[b_wft, b_m12], writes=[PSB[bk]])
                P.op("dve", lambda e, dc=dc, bk=bk: e.tensor_copy(out=wvg[:, dc, 768:1280], in_=ps[:, bk, :]),
                     reads=[PSB[bk]], writes=[b_wvg])

        xb = [sb("xb%d" % i, [128, 8, 512], F32, ls) for i in range(2)]
        b_xb = [Buf("xb%d" % i) for i in range(2)]
        sq = sb("sq", [128, 8, 512], BF16, ls)
        b_sq = Buf("sq")
        hb = [sb("hb%d" % i, [128, 8, 512], BF16, ls) for i in range(2)]
        b_hb = [Buf("hb%d" % i) for i in range(2)]
        rstd = sb("rstd", [128, 512], F32, ls)
        b_rstd = Buf("rstd")
        rk = [sb("rk%d" % i, [128, 2, 512], F32, ls) for i in range(2)]
        b_rk = [Buf("rk%d" % i) for i in range(2)]
        rq = sb("rq", [128, 2, NQ], F32, ls)
        b_rq = Buf("rq")
        t1 = [sb("t1_%d" % i, [128, 512], F32, ls) for i in range(2)]
        t2 = [sb("t2_%d" % i, [128, 512], F32, ls) for i in range(2)]
        b_t1 = [Buf("t1_%d" % i) for i in range(2)]
        b_t2 = [Buf("t2_%d" % i) for i in range(2)]
        kst = [sb("kst%d" % i, [128, NH, 512], BF16, ls) for i in range(2)]
        b_kst = [Buf("kst%d" % i) for i in range(2)]
        qst = [sb("qst0", [128, NH, 512], BF16, ls)] * 2
        b_qst = [Buf("qst0")] * 2
        vst = [sb("vst%d" % i, [128, 1280], BF16, ls) for i in range(2)]
        b_vst = [Buf("vst%d" % i) for i in range(2)]
        P.dma("sp", rq[:, :, :], ropeQ[:, :, :], "rq", writes=[b_rq])

        rot = [0]

        def rope_main(w, wbuf, hbt, hbbuf, hcols, n, h):
            r = rot[0] % 2
            rot[0] += 1
            ba = 1 + 2 * r
            for k in range(8):
                P.op("pe", lambda e, k=k: e.matmul(ps[:, ba, 0:n], lhsT=w[:, k, h * 128:(h + 1) * 128], rhs=hbt[:, k, hcols],
                                                   start=(k == 0), stop=(k == 7)), reads=[wbuf, hbbuf], writes=[PSB[ba]])
            P.op("act", lambda e: e.activation(out=kf[r][:, 0:n], in_=ps[:, ba, 0:n], func=AF.Copy), reads=[PSB[ba]], writes=[b_kf[r]])
            return r

        def rope_rest(r, n, cos_ap, sin_ap, tbufs, dst, dst_buf):
            bb = 2 + 2 * r
            P.op("pe", lambda e: e.matmul(ps[:, bb, 0:n], lhsT=permf[:, :], rhs=kf[r][:, 0:n], start=True, stop=True),
                 reads=[b_permf, b_kf[r]], writes=[PSB[bb]])
            P.op("dve", lambda e: e.tensor_tensor(out=t1[r][:, 0:n], in0=kf[r][:, 0:n], in1=cos_ap, op=ALU.mult),
                 reads=[b_kf[r]] + tbufs, writes=[b_t1[r]])
            P.op("dve", lambda e: e.tensor_tensor(out=t2[r][:, 0:n], in0=ps[:, bb, 0:n], in1=sin_ap, op=ALU.mult),
                 reads=[PSB[bb]] + tbufs, writes=[b_t2[r]])
            P.op("pool", lambda e: e.tensor_tensor(out=dst, in0=t1[r][:, 0:n], in1=t2[r][:, 0:n], op=ALU.add),
                 reads=[b_t1[r], b_t2[r]], writes=[dst_buf])

        class RopePipe:
            def __init__(self):
                self.pending = None

            def push(self, w, wbuf, hbt, hbbuf, hcols, n, cos_ap, sin_ap, tbufs, dst, dst_buf, h):
                r = rope_main(w, wbuf, hbt, hbbuf, hcols, n, h)
                self.flush()
                self.pending = (r, n, cos_ap, sin_ap, tbufs, dst, dst_buf)

            def flush(self):
                if self.pending is not None:
                    rope_rest(*self.pending)
                    self.pending = None

        rpipe = RopePipe()

        def rope_proj(w, wbuf, hbt, hbbuf, hcols, n, cos_ap, sin_ap, tbufs, dst, dst_buf, h):
            rpipe.push(w, wbuf, hbt, hbbuf, hcols, n, cos_ap, sin_ap, tbufs, dst, dst_buf, h)

        nblk = 9

        def blkinfo(blk):
            return blk * 512, (512 if blk < 8 else 256), blk % 2

        def load_norm(blk):
            t0, n, s = blkinfo(blk)
            P.dma("sp", xb[s][:, :, 0:n], xT[:, :, t0:t0 + n], "xb%d" % s, writes=[b_xb[s]])
            P.dma("sp", rk[s][:, :, 0:n], ropeK[:, :, t0:t0 + n], "rk%d" % s, writes=[b_rk[s]])
            if blk < 8:
                A, B, ABb = drv[:, 0, 0, :], drv[:, 0, 1, :], [b_drv]
            else:
                A, B, ABb = drvc[:, 0, :], drvc[:, 1, :], [b_drvc]
            rms_mod(xb[s][:, :, 0:n], n, [b_xb[s]], A, B, ABb, hb[s][:, :, 0:n], b_hb[s], xb[s][:, :, 0:n], b_xb[s],
                    sq[:, :, 0:n], b_sq, rstd[:, 0:n], b_rstd, 0)

        load_norm(0)
        for blk in range(nblk):
            t0, n, s = blkinfo(blk)
            ada_jobs(3)
            ada_dma(3)
            if blk + 1 < nblk:
                load_norm(blk + 1)
            ntile = n // 128
            for h in range(NH):
                rope_proj(wk, b_wk, hb[s], b_hb[s], slice(0, n), n, rk[s][:, 0, 0:n], rk[s][:, 1, 0:n], [b_rk[s]],
                          kst[s][:, h, 0:n], b_kst[s], h)
                if h < ntile:
                    j = h
                    vs = (blk * 4 + j) % 2
                    for ci, (c0, c1) in enumerate([(0, 512), (512, 1024), (1024, 1280)]):
                        for k in range(8):
                            P.op("pe", lambda e, k=k, j=j, ci=ci, c0=c0, c1=c1: e.matmul(
                                ps[:, 5 + ci, 0:c1 - c0], lhsT=hb[s][:, k, j * 128:(j + 1) * 128], rhs=wvg[:, k, c0:c1],
                                start=(k == 0), stop=(k == 7)), reads=[b_hb[s], b_wvg], writes=[PSB[5 + ci]])
                    P.op("act", lambda e, vs=vs: e.activation(out=vst[vs][:, 0:1024].rearrange("p (a b) -> p a b", a=2),
                                                              in_=ps[:, 5:7, :], func=AF.Copy),
                         reads=[PSB[5], PSB[6]], writes=[b_vst[vs]])
                    P.op("act", lambda e, vs=vs: e.activation(out=vst[vs][:, 1024:1280], in_=ps[:, 7, 0:256], func=AF.Copy),
                         reads=[PSB[7]], writes=[b_vst[vs]])
                    tix = blk * 4 + j
                    P.dma("pool", V_d[tix, :, :], vst[vs][:, 0:768], "vst%d" % vs, reads=[b_vst[vs]])
                    if blk < 8:
                        P.dma("pool", G_d[tix, :, :], vst[vs][:, 768:1280], "vstg%d" % vs, reads=[b_vst[vs]])
            rpipe.flush()
            P.dma("pool", KT_d[:, :, t0:t0 + n].rearrange("h p t -> p h t"), kst[s][:, :, 0:n], "kst%d" % s, reads=[b_kst[s]])
            if blk < 4:
                for h in range(NH):
                    rope_proj(wq, b_wq, hb[s], b_hb[s], slice(0, 512), 512, rq[:, 0, t0:t0 + 512], rq[:, 1, t0:t0 + 512], [b_rq],
                              qst[s][:, h, :], b_qst[s], h)
                rpipe.flush()
                P.dma("pool", QT_d[:, :, t0:t0 + 512].rearrange("h p t -> p h t"), qst[s][:, :, :], "qst0", reads=[b_qst[s]])
            if blk in (4, 7):
                hc = slice(0, 16) if blk == 4 else slice(496, 512)
                qc = 2064 if blk == 4 else 2048
                for h in range(NH):
                    rope_proj(wq, b_wq, hb[s], b_hb[s], hc, 16, rq[:, 0, qc:qc + 16], rq[:, 1, qc:qc + 16], [b_rq],
                              qst[s][:, h, 0:16], b_qst[s], h)
                rpipe.flush()
                P.dma("pool", QT_d[:, :, qc:qc + 16].rearrange("h p t -> p h t"), qst[s][:, :, 0:16], "qst0", reads=[b_qst[s]])
        P.barrier()

    assert ada_state["next"] == 24
    ws.close()
    if upto < 2:
        P.emit()
        gs.close()
        return nc

    X = sb("X", [128, 8, NQ], F32)
    ms = ExitStack()
    oall = sb("oall", [128, NH, NQ], F32, ms)
    b_oall = [Buf("oall%d" % i) for i in range(5)]
    with ExitStack() as ls:
        ktb = [sb("ktb%d" % i, [128, NT], BF16, ls) for i in range(2)]
        vtb = [sb("vtb%d" % i, [128, 34, 128], BF16, ls) for i in range(2)]
        qtb = [sb("qtb%d" % i, [128, NQ], BF16, ls) for i in range(2)]
        b_ktb = [Buf("ktb%d" % i) for i in range(2)]
        b_vtb = [Buf("vtb%d" % i) for i in range(2)]
        b_qtb = [Buf("qtb%d" % i) for i in range(2)]
        pT = [sb("pT%d" % i, [128, 2, 2, 512], BF16, ls) for i in range(3)]
        b_pT = [[Buf("pT%d_%d" % (i, j)) for j in range(2)] for i in range(3)]
        rs_sb = sb("rs_sb", [64, 512], F32, ls)
        b_rs = Buf("rs_sb")
        bc_sb = sb("bc_sb", [128, 2, 512], F32, ls)
        b_bc = Buf("bc_sb")
        o1 = sb("o1", [128, 512], F32, ls)
        b_o1 = Buf("o1")
        racc = sb("racc", [128, 2, 2, 512], F32, ls)
        b_racc = Buf("racc")
        onesf = sb("onesf", [128, 128], F32, ls)
        b_onesf = Buf("onesf")
        P.op("dve", lambda e: e.memset(onesf[:, :], 1.0), writes=[b_onesf])
        selc = sb("selc", [128, 2, 128], BF16, ls)
        b_selc = Buf("selc")
        P.op("dve", lambda e: e.memset(selc[:, :, :], 0.0), writes=[b_selc])
        for c_ in range(2):
            for j_ in range(2):
                g_ = 2 * j_ + c_
                P.op("dve", lambda e: e.memset(selc[32 * g_:32 * (g_ + 1), c_, :], 1.0 / 32.0), writes=[b_selc])
        rs_pe = sb("rs_pe", [128, 512], BF16, ls)
        b_rspe = Buf("rs_pe")
        raccb = sb("raccb", [128, 2, 2, 512], BF16, ls)
        b_raccb = Buf("raccb")
        o0 = sb("o0", [128, 512], F32, ls)
        b_o0 = Buf("o0")
        PE_PAIRS = (2, 5, 8, 11, 14)
        unit = 0
        QB2 = [(i * 416, 416) for i in range(5)]

        def make_fin(h, qi, q0, n):
            def fin_a():
                P.op("dve", lambda e: e.tensor_copy(out=o0[:, 0:n], in_=ps[:, 4, 0:n]), reads=[PSB[4]], writes=[b_o0])
                P.op("dve", lambda e: e.tensor_copy(out=o1[:, 0:n], in_=ps[:, 5, 0:n]), reads=[PSB[5]], writes=[b_o1])
                P.op("dve", lambda e: e.tensor_copy(out=rs_pe[:, 0:n], in_=ps[:, 7, 0:n]), reads=[PSB[7]], writes=[b_rspe])
                for c in range(2):
                    for j in range(2):
                        P.op("pe", lambda e: e.matmul(ps[:, 6 + c, 0:n], lhsT=ones[:, :], rhs=raccb[:, j, c, 0:n], start=(j == 0), stop=False),
                             reads=[b_ones, b_raccb], writes=[PSB[6 + c]])
                    P.op("pe", lambda e: e.matmul(ps[:, 6 + c, 0:n], lhsT=selc[:, c, :], rhs=rs_pe[:, 0:n], start=False, stop=True),
                         reads=[b_selc, b_rspe], writes=[PSB[6 + c]])

            def fin_b():
                for c in range(2):
                    P.op("act", lambda e: e.activation(out=bc_sb[:, c, 0:n], in_=ps[:, 6 + c, 0:n], func=AF.Ln), reads=[PSB[6 + c]], writes=[b_bc])
                P.op("act", lambda e: e.activation(out=bc_sb[:, :, 0:n], in_=bc_sb[:, :, 0:n], func=AF.Exp, scale=-1.0), reads=[b_bc], writes=[b_bc])
                P.op("dve", lambda e: e.tensor_scalar(out=bc_sb[:, 1, 0:n], in0=bc_sb[:, 1, 0:n], scalar1=lamt[:, 2:3], scalar2=None, op0=ALU.mult),
                     reads=[b_bc, b_lam], writes=[b_bc])
                P.op("pool", lambda e: e.tensor_tensor(out=oall[:, h, q0:q0 + n], in0=o0[:, 0:n], in1=bc_sb[:, 0, 0:n], op=ALU.mult),
                     reads=[b_o0, b_bc], writes=[b_oall[qi]])
                P.op("pool", lambda e: e.tensor_tensor(out=o1[:, 0:n], in0=o1[:, 0:n], in1=bc_sb[:, 1, 0:n], op=ALU.mult),
                     reads=[b_o1, b_bc], writes=[b_o1])
                P.op("pool", lambda e: e.tensor_tensor(out=oall[:, h, q0:q0 + n], in0=oall[:, h, q0:q0 + n], in1=o1[:, 0:n], op=ALU.add),
                     reads=[b_o1, b_oall[qi]], writes=[b_oall[qi]])
            return fin_a, fin_b

        pending_fin = None
        for h in range(NH):
            s = h % 2
            P.dma("sp", ktb[s][:, :], KT_d[h, :, :], "ktb%d" % s, writes=[b_ktb[s]])
            P.dma("sp", vtb[s][:, :, :], V_d[:, :, h * 128:(h + 1) * 128].rearrange("j p e -> p j e"), "vtb%d" % s, writes=[b_vtb[s]])
            P.dma("sp", qtb[s][:, :], QT_d[h, :, :], "qtb%d" % s, writes=[b_qtb[s]])
            for qi, (q0, n) in enumerate(QB2):
                pend = None
                for kp in range(17 + 1):
                    if kp < 17:
                        pb = unit % 3
                        unit += 1
                        for j in range(2):
                            kt = 2 * kp + j
                            for c in range(2):
                                P.op("pe", lambda e: e.matmul(
                                    ps[:, 2 * j + c, 0:n], lhsT=ktb[s][64 * c:64 * (c + 1), kt * 128:(kt + 1) * 128],
                                    rhs=qtb[s][64 * c:64 * (c + 1), q0:q0 + n], start=True, stop=True),
                                    reads=[b_ktb[s], b_qtb[s]], writes=[PSB[2 * j + c]])
                            P.op("act", lambda e: e.activation(out=pT[pb][:, j, :, 0:n], in_=ps[:, 2 * j:2 * j + 2, 0:n], func=AF.Exp),
                                 reads=[PSB[2 * j], PSB[2 * j + 1]], writes=[b_pT[pb][j]])
                    if kp == 0 and pending_fin is not None:
                        pending_fin[0]()
                    if kp == 1 and pending_fin is not None:
                        pending_fin[1]()
                        pending_fin = None
                    if pend is not None:
                        pkp, ppb = pend
                        for j in range(2):
                            pkt = 2 * pkp + j
                            for c in range(2):
                                P.op("pe", lambda e: e.matmul(
                                    ps[:, 4 + c, 0:n], lhsT=vtb[s][:, pkt, :], rhs=pT[ppb][:, j, c, 0:n],
                                    start=(pkt == 0), stop=(pkt == 33)), reads=[b_vtb[s], b_pT[ppb][j]], writes=[PSB[4 + c]])
                        if pkp in PE_PAIRS:
                            for j in range(2):
                                for c in range(2):
                                    g = 2 * j + c
                                    P.op("pe", lambda e: e.matmul(
                                        ps[32 * g:32 * (g + 1), 7, 0:n], lhsT=ones[:, 0:32], rhs=pT[ppb][:, j, c, 0:n],
                                        start=(pkp == PE_PAIRS[0]), stop=(pkp == PE_PAIRS[-1]), tile_position=(0, 32 * g)),
                                        reads=[b_ones, b_pT[ppb][j]], writes=[PSB[7]])
                        elif pkp == 0:
                            P.op("dve", lambda e: e.tensor_copy(out=racc[:, :, :, 0:n], in_=pT[ppb][:, :, :, 0:n]),
                                 reads=b_pT[ppb], writes=[b_racc])
                        elif pkp == 16:
                            P.op("dve", lambda e: e.tensor_tensor(out=raccb[:, :, :, 0:n], in0=racc[:, :, :, 0:n], in1=pT[ppb][:, :, :, 0:n], op=ALU.add),
                                 reads=b_pT[ppb] + [b_racc], writes=[b_raccb])
                        else:
                            P.op("dve", lambda e: e.tensor_tensor(out=racc[:, :, :, 0:n], in0=racc[:, :, :, 0:n], in1=pT[ppb][:, :, :, 0:n], op=ALU.add),
                                 reads=b_pT[ppb] + [b_racc], writes=[b_racc])
                    pend = (kp, pb) if kp < 17 else None
                pending_fin = make_fin(h, qi, q0, n)
        pending_fin[0]()
        pending_fin[1]()
        if debug:
            P.dma("sp", dbgO[:, :, :], oall[:, :, :], "dbg", reads=b_oall)
        P.barrier()

    if upto < 3:
        P.emit()
        ms.close()
        gs.close()
        return nc

    with ExitStack() as ls:
        gsb = sb("gsb", [128, 32, 512], BF16, ls)
        b_gsb = Buf("gsb")
        wo = sb("wo", [128, 8, D], BF16, ls)
        b_wo = Buf("wo")
        P.dma("sp", gsb[:, :, :], G_d.rearrange("j p n -> p j n"), "gsb", writes=[b_gsb])
        P.dma("pool", wo[:, :, :], w_out.rearrange("(c p) n -> p c n", p=128), "wo", writes=[b_wo])
        dpc = [sb("dpc%d" % i, [128, 4, 512], BF16, ls) for i in range(4)]
        b_dpc = [Buf("dpc%d" % i) for i in range(4)]
        osq = [sb("osq%d" % i, [128, 512], BF16, ls) for i in range(2)]
        b_osq = [Buf("osq%d" % i) for i in range(2)]
        ors = [sb("ors%d" % i, [128, 512], F32, ls) for i in range(2)]
        b_ors = [Buf("ors%d" % i) for i in range(2)]
        otb = [sb("otb%d" % i, [128, 8, 512], BF16, ls) for i in range(2)]
        b_otb = [Buf("otb%d" % i) for i in range(2)]
        ring = 0

        def wout_group(qi_, q0_, n_, s_, g):
            bk = 4 + g % 4
            for c in range(8):
                P.op("pe", lambda e: e.matmul(ps[:, bk, 0:n_], lhsT=wo[:, c, g * 128:(g + 1) * 128], rhs=otb[s_][:, c, 0:n_],
                                              start=(c == 0), stop=(c == 7)), reads=[b_wo, b_otb[s_]], writes=[PSB[bk]])
            P.op("dve", lambda e: e.scalar_tensor_tensor(out=X[:, g, q0_:q0_ + n_], in0=ps[:, bk, 0:n_], scalar=drv[:, 0, 2, g:g + 1], in1=X[:, g, q0_:q0_ + n_],
                                                         op0=ALU.mult, op1=ALU.add), reads=[PSB[bk], b_drv, b_drv2, b_X[qi_]], writes=[b_X[qi_]])

        pending = []
        for qi, (q0, n) in enumerate(QB):
            s = qi % 2
            if n == 512:
                P.dma("sp", X[:, :, q0:q0 + 512], xT[:, :, q0:q0 + 512], "xr%d" % s, writes=[b_X[qi]])
            else:
                P.dma("sp", X[:, :, q0:q0 + 16], xT[:, :, 4080:4096], "xr%d" % s, writes=[b_X[qi]])
                P.dma("sp", X[:, :, q0 + 16:q0 + 32], xT[:, :, 2048:2064], "xr%d" % s, writes=[b_X[qi]])
            for h in range(NH):
                bk = h % 2
                P.op("act", lambda e: e.activation(out=osq[bk][:, 0:n], in_=oall[:, h, q0:q0 + n], func=AF.Square), reads=[b_oall[qi]], writes=[b_osq[bk]])
                P.op("pe", lambda e: e.matmul(ps[:, bk, 0:n], lhsT=ones[:, :], rhs=osq[bk][:, 0:n], start=True, stop=True),
                     reads=[b_ones, b_osq[bk]], writes=[PSB[bk]])
                P.op("act", lambda e: e.activation(out=ors[bk][:, 0:n], in_=ps[:, bk, 0:n], func=AF.Ln, bias=epsb[:, 0:1], scale=1.0 / 128),
                     reads=[PSB[bk], b_eps], writes=[b_ors[bk]])
                P.op("act", lambda e: e.activation(out=ors[bk][:, 0:n], in_=ors[bk][:, 0:n], func=AF.Exp, scale=-0.5), reads=[b_ors[bk]], writes=[b_ors[bk]])
                P.op("dve", lambda e: e.scalar_tensor_tensor(out=otb[s][:, h, 0:n], in0=oall[:, h, q0:q0 + n], scalar=lamt[:, 3:4], in1=ors[bk][:, 0:n],
                                                             op0=ALU.mult, op1=ALU.mult), reads=[b_oall[qi], b_lam, b_ors[bk]], writes=[b_otb[s]])
            step = 0
            npiece = 0
            for tab, src in ((0, dftC), (1, dftS)):
                for pc in range(8):
                    r = ring % 4
                    ring += 1
                    P.dma("sp", dpc[r][:, :, 0:n], src[pc * 4:(pc + 1) * 4, :, q0:q0 + n].rearrange("j p n -> p j n"), "dpc%d" % r, writes=[b_dpc[r]])
                    for j in range(4):
                        tc_ = pc * 4 + j
                        for m in range(2):
                            P.op("pe", lambda e: e.matmul(
                                ps[:, 2 + m, 0:n], lhsT=gsb[:, tc_, tab * 256 + m * 128: tab * 256 + (m + 1) * 128], rhs=dpc[r][:, j, 0:n],
                                start=(step == 0), stop=(step == 63)), reads=[b_gsb, b_dpc[r]], writes=[PSB[2 + m]])
                        step += 1
                    npiece += 1
                    if npiece % 2 == 0 and pending:
                        wout_group(*pending.pop(0))
            for m in range(2):
                P.op("act", lambda e: e.activation(out=otb[s][:, 6 + m, 0:n], in_=ps[:, 2 + m, 0:n], func=AF.Copy), reads=[PSB[2 + m]], writes=[b_otb[s]])
            while pending:
                wout_group(*pending.pop(0))
            pending = [(qi, q0, n, s, g) for g in range(8)]
        while pending:
            wout_group(*pending.pop(0))
        if debug and upto == 3:
            P.dma("sp", dbgX[:, :, :], X[:, :, :], "dbg", reads=b_X)
        P.barrier()
    ms.close()

    b_Hs = [Buf("Hs%d" % i) for i in range(5)]
    PIECES = [4, 4, 4, 4, 3, 3]

    def make_norm(li, which, ls):
        Hs = sb("Hs", [128, 8, NQ], BF16, ls)
        tmp = sb("nrm_tmp", [128, 8, 512], F32, ls)
        b_tmp = Buf("nrm_tmp")
        sq = sb("nrm_sq", [128, 8, 512], BF16, ls)
        b_sq = Buf("nrm_sq")
        rstd = sb("nrm_rstd", [128, 512], F32, ls)
        b_rstd = Buf("nrm_rstd")
        a_i, b_i = (0, 1) if which == "mix" else (3, 4)

        def norm_a(qi):
            q0, n = QB[qi]
            rms_a(X[:, :, q0:q0 + n], [b_X[qi]], sq[:, :, 0:n], b_sq)

        def norm_b(qi, psb, part=0):
            q0, n = QB[qi]
            if part in (0, 1):
                rms_b1(X[:, :, q0:q0 + n], n, [b_X[qi]], tmp[:, :, 0:n], b_tmp, sq[:, :, 0:n], b_sq, rstd[:, 0:n], b_rstd, psb)
            if part in (0, 2):
                rms_b2(drv[:, li, a_i, :], drv[:, li, b_i, :], [b_drv, b_drv2], Hs[:, :, q0:q0 + n], b_Hs[qi], tmp[:, :, 0:n], b_tmp)
        norm_b.bufs = (tmp, b_tmp, sq, b_sq, rstd, b_rstd)
        return Hs, norm_a, norm_b

    def ffn(li, nblocks, final=False):
        with ExitStack() as ls:
            Hs, norm_a, norm_b = make_norm(li, "ffn", ls)
            if final:
                ob = [sb("ob0", [128, 8, 512], F32, ls)] * 2
                b_ob = [Buf("ob0")] * 2
                f_tmp, fb_tmp, f_sq, fb_sq, f_rstd, fb_rstd = norm_b.bufs
            norm_a(0)
            norm_b(0, 7)
            w1p = [sb("w1p%d" % i, [128, 8, 512], BF16, ls) for i in range(2)]
            w3p = [sb("w3p%d" % i, [128, 8, 512], BF16, ls) for i in range(2)]
            w2p = [sb("w2p%d" % i, [128, 4, D], BF16, ls) for i in range(2)]
            b_w1p = [Buf("w1p%d" % i) for i in range(2)]
            b_w3p = [Buf("w3p%d" % i) for i in range(2)]
            b_w2p = [Buf("w2p%d" % i) for i in range(2)]
            hid = [sb("hid%d" % i, [128, 4, 512], BF16, ls) for i in range(2)]
            b_hid = [Buf("hid%d" % i) for i in range(2)]
            sl = [sb("sl%d" % i, [128, 512], F32, ls) for i in range(2)]
            b_sl = [Buf("sl%d" % i) for i in range(2)]
            w1v = ffn_w1.rearrange("l (c p) n -> l p c n", p=128)
            w3v = ffn_w3.rearrange("l (c p) n -> l p c n", p=128)
            w2v = ffn_w2.rearrange("l (c p) n -> l p c n", p=128)
            c0 = 0
            it = 0
            for pi, m in enumerate(PIECES):
                s = pi % 2
                P.dma("pool", w1p[s][:, :, 0:128 * m], w1v[li, :, :, c0 * 128:(c0 + m) * 128], "w1p%d" % s, writes=[b_w1p[s]])
                P.dma("pool", w3p[s][:, :, 0:128 * m], w3v[li, :, :, c0 * 128:(c0 + m) * 128], "w3p%d" % s, writes=[b_w3p[s]])
                P.dma("pool", w2p[s][:, 0:m, :], w2v[li, :, c0:c0 + m, :], "w2p%d" % s, writes=[b_w2p[s]])
                for qi in range(nblocks):
                    q0, n = QB[qi]
                    hs = it % 2
                    it += 1
                    pipe_norm = (pi == 0 and qi + 1 < nblocks)
                    if pipe_norm:
                        norm_a(qi + 1)
                    for j in range(m):
                        if pipe_norm and j == 1:
                            norm_b(qi + 1, 7, part=1)
                        if pipe_norm and j == m - 1:
                            norm_b(qi + 1, 7, part=2)
                        bs = (it + j) % 2
                        for k in range(8):
                            P.op("pe", lambda e, j=j, k=k, bs=bs: e.matmul(ps[:, 2 * bs, 0:n], lhsT=w1p[s][:, k, j * 128:(j + 1) * 128], rhs=Hs[:, k, q0:q0 + n],
                                                                          start=(k == 0), stop=(k == 7)), reads=[b_w1p[s], b_Hs[qi]], writes=[PSB[2 * bs]])
                        for k in range(8):
                            P.op("pe", lambda e, j=j, k=k, bs=bs: e.matmul(ps[:, 2 * bs + 1, 0:n], lhsT=w3p[s][:, k, j * 128:(j + 1) * 128], rhs=Hs[:, k, q0:q0 + n],
                                                                          start=(k == 0), stop=(k == 7)), reads=[b_w3p[s], b_Hs[qi]], writes=[PSB[2 * bs + 1]])
                        P.op("act", lambda e, bs=bs: e.activation(out=sl[bs][:, 0:n], in_=ps[:, 2 * bs, 0:n], func=AF.Silu), reads=[PSB[2 * bs]], writes=[b_sl[bs]])
                        P.op("dve", lambda e, j=j, bs=bs, hs=hs: e.tensor_tensor(out=hid[hs][:, j, 0:n], in0=ps[:, 2 * bs + 1, 0:n], in1=sl[bs][:, 0:n], op=ALU.mult),
                             reads=[PSB[2 * bs + 1], b_sl[bs]], writes=[b_hid[hs]])
                    for g in range(8):
                        bk = 4 + g % 3
                        for j in range(m):
                            P.op("pe", lambda e, g=g, j=j, bk=bk, hs=hs: e.matmul(ps[:, bk, 0:n], lhsT=w2p[s][:, j, g * 128:(g + 1) * 128], rhs=hid[hs][:, j, 0:n],
                                                                                 start=(j == 0), stop=(j == m - 1)), reads=[b_w2p[s], b_hid[hs]], writes=[PSB[bk]])
                        P.op("dve", lambda e, g=g, bk=bk: e.scalar_tensor_tensor(out=X[:, g, q0:q0 + n], in0=ps[:, bk, 0:n], scalar=drv[:, li, 5, g:g + 1], in1=X[:, g, q0:q0 + n],
                                                                                 op0=ALU.mult, op1=ALU.add), reads=[PSB[bk], b_drv, b_drv2, b_X[qi]], writes=[b_X[qi]])
                    if final and pi == len(PIECES) - 1:
                        so = qi % 2
                        rms_mod(X[:, :, q0:q0 + 512], 512, [b_X[qi]], v("finalg"), None, [b_vec], ob[so][:, :, :], b_ob[so],
                                f_tmp[:, :, :], fb_tmp, f_sq[:, :, :], fb_sq, f_rstd[:, :], fb_rstd, 7)
                        P.dma("sp", outT[:, :, q0:q0 + 512], ob[so][:, :, :], "ob0", reads=[b_ob[so]])
                c0 += m
            P.barrier()

    if upto >= 4:
        ffn(0, 5)
        if debug and upto == 4:
            P.dma("sp", dbgX[:, :, :], X[:, :, :], "dbg", reads=b_X)
            P.barrier()

    if upto >= 5:
        with ExitStack() as ls:
            U = sb("U", [128, 8, NQ], BF16, ls)
            b_U = Buf("U")
            with ExitStack() as l2:
                Hs, norm_a, norm_b = make_norm(1, "mix", l2)
                norm_a(0)
                norm_b(0, 7)
                pw1 = sb("pw1", [128, 8, 2 * D], BF16, l2)
                b_pw1 = Buf("pw1")
                P.dma("pool", pw1[:, :, :], pw1_w.rearrange("(c p) n -> p c n", p=128), "pw1", writes=[b_pw1])
                sgt = [sb("sgt%d" % i, [128, 512], F32, l2) for i in range(2)]
                b_sgt = [Buf("sgt%d" % i) for i in range(2)]
                pb = VC["pw1b"][0]
                it = 0
                for qi, (q0, n) in enumerate(QB):
                    if qi + 1 < 5:
                        norm_a(qi + 1)
                    for j in range(8):
                        if qi + 1 < 5 and j == 1:
                            norm_b(qi + 1, 7, part=1)
                        if qi + 1 < 5 and j == 5:
                            norm_b(qi + 1, 7, part=2)
                        bs = it % 2
                        it += 1
                        for k in range(8):
                            P.op("pe", lambda e, j=j, k=k, bs=bs: e.matmul(ps[:, 2 * bs, 0:n], lhsT=pw1[:, k, j * 128:(j + 1) * 128], rhs=Hs[:, k, q0:q0 + n],
                                                                          start=(k == 0), stop=(k == 7)), reads=[b_pw1, b_Hs[qi]], writes=[PSB[2 * bs]])
                        for k in range(8):
                            P.op("pe", lambda e, j=j, k=k, bs=bs: e.matmul(ps[:, 2 * bs + 1, 0:n], lhsT=pw1[:, k, D + j * 128:D + (j + 1) * 128], rhs=Hs[:, k, q0:q0 + n],
                                                                          start=(k == 0), stop=(k == 7)), reads=[b_pw1, b_Hs[qi]], writes=[PSB[2 * bs + 1]])
                        P.op("act", lambda e, j=j, bs=bs: e.activation(out=sgt[bs][:, 0:n], in_=ps[:, 2 * bs + 1, 0:n], func=AF.Sigmoid, bias=vec[:, pb + 8 + j:pb + 9 + j]),
                             reads=[PSB[2 * bs + 1], b_vec], writes=[b_sgt[bs]])
                        if n == 512:
                            P.op("dve", lambda e, j=j, bs=bs: e.scalar_tensor_tensor(out=U[:, j, 16 + q0:16 + q0 + 512], in0=ps[:, 2 * bs, 0:512], scalar=vec[:, pb + j:pb + j + 1],
                                                                                    in1=sgt[bs][:, 0:512], op0=ALU.add, op1=ALU.mult),
                                 reads=[PSB[2 * bs], b_vec, b_sgt[bs]], writes=[b_U])
                        else:
                            for (a0, u0, mk) in ((0, 0, "ml"), (16, 2064, "mr")):
                                P.op("dve", lambda e, j=j, bs=bs, a0=a0, u0=u0: e.scalar_tensor_tensor(
                                    out=U[:, j, u0:u0 + 16], in0=ps[:, 2 * bs, a0:a0 + 16], scalar=vec[:, pb + j:pb + j + 1],
                                    in1=sgt[bs][:, a0:a0 + 16], op0=ALU.add, op1=ALU.mult), reads=[PSB[2 * bs], b_vec, b_sgt[bs]], writes=[b_U])
                                P.op("dve", lambda e, j=j, u0=u0, mk=mk: e.tensor_scalar(out=U[:, j, u0:u0 + 16], in0=U[:, j, u0:u0 + 16], scalar1=v(mk), scalar2=None, op0=ALU.mult),
                                     reads=[b_U, b_vec], writes=[b_U])
                if debug and upto == 5:
                    P.dma("sp", dbgU[:, :, :], U[:, :, :], "dbg", reads=[b_U])
                P.barrier()
            pw2 = sb("pw2", [128, 8, D], BF16, ls)
            b_pw2 = Buf("pw2")
            P.dma("pool", pw2[:, :, :], pw2_w.rearrange("(c p) n -> p c n", p=128), "pw2", writes=[b_pw2])
            dg = [sb("dg%d" % i, [128, 31, 128], BF16, ls) for i in range(2)]
            b_dg = [Buf("dg%d" % i) for i in range(2)]
            vc = [sb("vc%d" % i, [128, 8, 512], F32, ls) for i in range(2)]
            b_vc = [Buf("vc%d" % i) for i in range(2)]
            vcb = sb("vcb", [128, 8, 512], BF16, ls)
            b_vcb = Buf("vcb")
            vsq = sb("vsq", [128, 8, 512], BF16, ls)
            b_vsq = Buf("vsq")
            mu = sb("mu", [128, 512], F32, ls)
            b_mu = Buf("mu")
            var = sb("var", [128, 512], F32, ls)
            b_var = Buf("var")
            sv, b_sv = vcb, b_vcb
            dwo = VC["dww"][0]
            dbo = VC["dwb"][0]
            lg = VC["lng"][0]
            lb = VC["lnb"][0]
            dstate = {"it": 0, "built": {}}

            def diag_build(qi, j):
                if (qi, j) in dstate["built"]:
                    return dstate["built"][(qi, j)]
                ds = dstate["it"] % 2
                dstate["it"] += 1
                P.op("dve", lambda e: e.tensor_tensor(out=dg[ds][:, :, :], in0=ident[:, :].unsqueeze(1).to_broadcast([128, 31, 128]),
                                                      in1=vec[:, dwo + j * 31:dwo + (j + 1) * 31].unsqueeze(2).to_broadcast([128, 31, 128]), op=ALU.mult),
                     reads=[b_ident, b_vec], writes=[b_dg[ds]])
                dstate["built"][(qi, j)] = ds
                return ds

            def conv(qi, chunks=range(8)):
                q0 = qi * 512
                vs_ = qi % 2
                for j in chunks:
                    ds = diag_build(qi, j)
                    bk = j % 2
                    for tap in range(31):
                        P.op("pe", lambda e: e.matmul(ps[:, bk, :], lhsT=dg[ds][:, tap, :], rhs=U[:, j, q0 + tap + 1:q0 + tap + 1 + 512],
                                                      start=(tap == 0), stop=(tap == 30)), reads=[b_dg[ds], b_U], writes=[PSB[bk]])
                    P.op("act", lambda e: e.activation(out=vc[vs_][:, j, :], in_=ps[:, bk, :], func=AF.Identity, bias=vec[:, dbo + j:dbo + j + 1]),
                         reads=[PSB[bk], b_vec], writes=[b_vc[vs_]])

            def ln_a1(qi):
                vs_ = qi % 2
                P.op("act", lambda e: e.activation(out=vsq[:, :, :], in_=vc[vs_][:, :, :], func=AF.Square), reads=[b_vc[vs_]], writes=[b_vsq])
                P.op("dve", lambda e: e.tensor_copy(out=vcb[:, :, :], in_=vc[vs_][:, :, :]), reads=[b_vc[vs_]], writes=[b_vcb])

            def ln_a2(qi):
                for c in range(8):
                    P.op("pe", lambda e: e.matmul(ps[:, 2, :], lhsT=ones[:, :], rhs=vcb[:, c, :], start=(c == 0), stop=(c == 7)), reads=[b_ones, b_vcb], writes=[PSB[2]])
                for c in range(8):
                    P.op("pe", lambda e: e.matmul(ps[:, 3, :], lhsT=ones[:, :], rhs=vsq[:, c, :], start=(c == 0), stop=(c == 7)), reads=[b_ones, b_vsq], writes=[PSB[3]])

            def ln_b(qi):
                vs_ = qi % 2
                P.op("act", lambda e: e.activation(out=mu[:, :], in_=ps[:, 2, :], func=AF.Copy, scale=1.0 / D), reads=[PSB[2]], writes=[b_mu])
                P.op("dve", lambda e: e.tensor_tensor(out=var[:, :], in0=mu[:, :], in1=mu[:, :], op=ALU.mult), reads=[b_mu], writes=[b_var])
                P.op("dve", lambda e: e.scalar_tensor_tensor(out=var[:, :], in0=ps[:, 3, :], scalar=1.0 / D, in1=var[:, :], op0=ALU.mult, op1=ALU.subtract),
                     reads=[PSB[3], b_var], writes=[b_var])
                P.op("act", lambda e: e.activation(out=var[:, :], in_=var[:, :], func=AF.Ln, bias=epsb[:, 0:1]), reads=[b_var, b_eps], writes=[b_var])
                P.op("act", lambda e: e.activation(out=var[:, :], in_=var[:, :], func=AF.Exp, scale=-0.5), reads=[b_var], writes=[b_var])
                P.op("dve", lambda e: e.tensor_tensor(out=vc[vs_][:, :, :], in0=vc[vs_][:, :, :], in1=mu[:, :].unsqueeze(1).to_broadcast([128, 8, 512]), op=ALU.subtract),
                     reads=[b_vc[vs_], b_mu], writes=[b_vc[vs_]])
                P.op("dve", lambda e: e.tensor_tensor(out=vc[vs_][:, :, :], in0=vc[vs_][:, :, :], in1=var[:, :].unsqueeze(1).to_broadcast([128, 8, 512]), op=ALU.mult),
                     reads=[b_vc[vs_], b_var], writes=[b_vc[vs_]])
                for c in range(8):
                    P.op("act", lambda e: e.activation(out=sv[:, c, :], in_=vc[vs_][:, c, :], func=AF.Silu, bias=vec[:, lb + c:lb + c + 1], scale=vec[:, lg + c:lg + c + 1]),
                         reads=[b_vc[vs_], b_vec], writes=[b_sv])

            def pw2f(qi):
                q0 = qi * 512
                for g in range(8):
                    bk = 4 + g % 4
                    for c in range(8):
                        P.op("pe", lambda e: e.matmul(ps[:, bk, :], lhsT=pw2[:, c, g * 128:(g + 1) * 128], rhs=sv[:, c, :], start=(c == 0), stop=(c == 7)),
                             reads=[b_pw2, b_sv], writes=[PSB[bk]])
                    P.op("dve", lambda e: e.scalar_tensor_tensor(out=X[:, g, q0:q0 + 512], in0=ps[:, bk, :], scalar=drv[:, 1, 2, g:g + 1], in1=X[:, g, q0:q0 + 512],
                                                                 op0=ALU.mult, op1=ALU.add), reads=[PSB[bk], b_drv, b_drv2, b_X[qi]], writes=[b_X[qi]])
                    P.op("dve", lambda e: e.tensor_scalar(out=X[:, g, q0:q0 + 512], in0=X[:, g, q0:q0 + 512], scalar1=drv[:, 1, 6, g:g + 1], scalar2=None, op0=ALU.add),
                         reads=[b_drv, b_drv2, b_X[qi]], writes=[b_X[qi]])

            conv(0)
            for qi in range(4):
                ln_a1(qi)
                if qi + 1 < 4:
                    conv(qi + 1, range(0, 3))
                ln_a2(qi)
                if qi + 1 < 4:
                    diag_build(qi + 1, 3)
                    diag_build(qi + 1, 4)
                ln_b(qi)
                if qi + 1 < 4:
                    conv(qi + 1, range(3, 8))
                pw2f(qi)
            if debug and upto == 5:
                P.dma("sp", dbgX[:, :, :], X[:, :, :], "dbg", reads=b_X)
            P.barrier()

    if upto >= 6:
        ffn(1, 4, final=True)

    P.emit()
    gs.close()
    return nc


def _const_tables(half):
    pos_tok = np.concatenate([(np.arange(SEQ) + OWN * half) % SEQ])
    qpos = np.concatenate([np.arange(OWN), np.arange(4080, 4096), np.arange(2048, 2064)])
    qtok = pos_tok[qpos]
    inv = (10000.0 ** (-np.arange(0, 32, 2, dtype=np.float32) / np.float32(32))).astype(np.float32)

    def tabs(tok):
        row = (tok // 64).astype(np.float32)
        col = (tok % 64).astype(np.float32)
        ang = np.concatenate([row[:, None] * inv, col[:, None] * inv], axis=-1).astype(np.float32)
        return np.cos(ang).astype(np.float32), np.sin(ang).astype(np.float32)

    pp = np.arange(128)
    dd = pp % 64
    ii = dd % 32
    sign = np.where(dd < 32, -1.0, 1.0).astype(np.float32)
    ck, sk = tabs(pos_tok)
    ropeK = np.zeros((128, 2, NT), np.float32)
    ropeK[:, 0, :SEQ] = ck[:, ii].T
    ropeK[:, 1, :SEQ] = sk[:, ii].T * sign[:, None]
    ropeK[:, 0, SEQ:] = 1.0
    cq, sq = tabs(qtok)
    ropeQ = np.zeros((128, 2, NQ), np.float32)
    ropeQ[:, 0, :] = cq[:, ii].T * 0.125
    ropeQ[:, 1, :] = sq[:, ii].T * sign[:, None] * 0.125
    prod = (pos_tok[:, None].astype(np.int64) * qtok[None, :].astype(np.int64)) % SEQ
    angd = 2.0 * np.pi * prod / SEQ
    dftC = (np.cos(angd) / 64.0).astype(np.float32).reshape(32, 128, NQ).astype(ml_dtypes.bfloat16)
    dftS = (-np.sin(angd) / 64.0).astype(np.float32).reshape(32, 128, NQ).astype(ml_dtypes.bfloat16)
    return ropeK, ropeQ, dftC, dftS


def _ccss():
    l = np.arange(64)
    a = 2.0 * np.pi * ((l[:, None] * l[None, :]) % 64) / 64.0
    cc = np.cos(a) / 8.0
    ss = np.sin(a) / 8.0
    out = np.zeros((256, 512), np.float32)
    for g in range(4):
        out[g * 64:(g + 1) * 64, g * 64:(g + 1) * 64] = cc
        out[g * 64:(g + 1) * 64, 256 + g * 64:256 + (g + 1) * 64] = ss
    return out.astype(ml_dtypes.bfloat16)


def _chunkT(vec1d):
    return np.ascontiguousarray(vec1d.reshape(-1, 128).T)


def prep_inputs(inp):
    f32 = np.float32
    x = np.asarray(inp["x"], f32)
    ctx = np.asarray(inp["ctx"], f32)
    w_in = np.asarray(inp["ev_w_in"], f32)[0]
    wq = w_in[:, 0:768]
    wk = w_in[:, 768:1536]
    wv = w_in[:, 1536:2304]
    wf = w_in[:, 2304:2560]
    swap = np.concatenate([np.arange(64 * i + 32, 64 * i + 64).tolist() + np.arange(64 * i, 64 * i + 32).tolist() for i in range(12)]).astype(np.int64)
    w_ext = np.ascontiguousarray(np.concatenate([wq, wk, wv], axis=1))
    permf = np.zeros((128, 128), f32)
    permf[swap[:128], np.arange(128)] = 1.0
    w_fT = np.ascontiguousarray(wf.T)
    fw = np.asarray(inp["ev_fourier_w"], f32)[0]
    fw_bd = np.zeros((256, 256), f32)
    for g in range(4):
        fw_bd[g * 64:(g + 1) * 64, g * 64:(g + 1) * 64] = fw[g]
    shared = {
        "ada_w": np.ascontiguousarray(np.asarray(inp["ada_w"], f32)),
        "w_ext": w_ext, "permf": permf, "w_fT": w_fT, "fw_bd": fw_bd, "ccss": _ccss(),
        "identb": np.eye(128, dtype=f32).astype(ml_dtypes.bfloat16),
        "w_out": np.ascontiguousarray(np.asarray(inp["ev_w_out"], f32)[0]),
        "ffn_w1": np.ascontiguousarray(np.asarray(inp["ffn_w1"], f32)),
        "ffn_w3": np.ascontiguousarray(np.asarray(inp["ffn_w3"], f32)),
        "ffn_w2": np.ascontiguousarray(np.asarray(inp["ffn_w2"], f32)),
        "pw1_w": np.ascontiguousarray(np.asarray(inp["od_pw1_w"], f32)[0]),
        "pw2_w": np.ascontiguousarray(np.asarray(inp["od_pw2_w"], f32)[0]),
    }
    consts = [_const_tables(h) for h in range(2)]
    vbase = np.zeros((128, NV), f32)

    def put(name, arr):
        o, w = VC[name]
        assert arr.shape == (128, w), (name, arr.shape)
        vbase[:, o:o + w] = arr

    put("ada_b0", _chunkT(np.asarray(inp["ada_b"], f32)[0]))
    put("ada_b1", _chunkT(np.asarray(inp["ada_b"], f32)[1]))
    put("mixg0", _chunkT(np.asarray(inp["mix_norm_g"], f32)[0]))
    put("mixg1", _chunkT(np.asarray(inp["mix_norm_g"], f32)[1]))
    put("ffng0", _chunkT(np.asarray(inp["ffn_norm_g"], f32)[0]))
    put("ffng1", _chunkT(np.asarray(inp["ffn_norm_g"], f32)[1]))
    put("finalg", _chunkT(np.asarray(inp["final_g"], f32)))
    put("pw1b", _chunkT(np.asarray(inp["od_pw1_b"], f32)[0]))
    put("dwb", _chunkT(np.asarray(inp["od_dw_b"], f32)[0]))
    put("lng", _chunkT(np.asarray(inp["od_ln_g"], f32)[0]))
    put("lnb", _chunkT(np.asarray(inp["od_ln_b"], f32)[0]))
    put("pw2b", _chunkT(np.asarray(inp["od_pw2_b"], f32)[0]))
    put("sublng", np.asarray(inp["ev_subln_g"], f32)[0].reshape(128, 1))
    dw = np.asarray(inp["od_dw_w"], f32)[0]
    put("dww", np.ascontiguousarray(dw.T.reshape(8, 128, 31).transpose(1, 0, 2).reshape(128, 248)))
    lamrow = np.concatenate([np.asarray(inp[k], f32)[0] for k in ("ev_lambda_q1", "ev_lambda_k1", "ev_lambda_q2", "ev_lambda_k2")])
    put("lam", np.broadcast_to(lamrow[None, :], (128, 256)))
    c_ctx = np.asarray(inp["c_ctx"], f32)
    in_maps = []
    for core in range(8):
        b, half = core // 2, core % 2
        seq = np.roll(x[b], -OWN * half, axis=0)
        full = np.concatenate([seq, ctx[b]], axis=0)
        xT = np.ascontiguousarray(full.T.reshape(8, 128, NT).transpose(1, 0, 2))
        cvec = np.zeros((128, 16), f32)
        cvec[:, 0::2] = _chunkT(np.asarray(inp["c"], f32)[b])
        cvec[:, 1::2] = _chunkT(c_ctx)
        vv = vbase.copy()
        vv[:, VC["ml"][0]] = 1.0 if half == 1 else 0.0
        vv[:, VC["mr"][0]] = 1.0 if half == 0 else 0.0
        ropeK, ropeQ, dftC, dftS = consts[half]
        m = dict(shared)
        m.update({"xT": xT, "cvec": cvec, "vecs": vv, "ropeK": ropeK, "ropeQ": ropeQ, "dftC": dftC, "dftS": dftS})
        in_maps.append(m)
    return in_maps


_NC_CACHE = {}


def kernel(**inputs):
    in_maps = prep_inputs(inputs)
    if "nc" not in _NC_CACHE:
        _NC_CACHE["nc"] = build_program()
    nc = _NC_CACHE["nc"]
    res = run_bass_kernel_spmd(nc, in_maps, core_ids=list(range(8)))
    out = np.zeros((4, SEQ, D), np.float32)
    for core in range(8):
        b, half = core // 2, core % 2
        oT = np.asarray(res.results[core]["outT"], np.float32)
        out[b, half * OWN:(half + 1) * OWN, :] = oT.transpose(2, 1, 0).reshape(OWN, D)
    return out
```
